# Optimizing a Trainium2 kernel written in Bass

```python
import jax, jax.numpy as jnp
from jax import lax
import numpy as np

D_MODEL = 2048
BATCH = 16
SEQ = 2048
DEPTH = 1
DEC_BATCH = 16
DEC_SEQ = 64
PAST_LEN = 1024

CHUNK = 64
HEAD_DIM = 64
N_HEADS = 16
N_KV_HEADS = 4
GROUP = N_HEADS // N_KV_HEADS
ATTN_WIDTH = N_HEADS * HEAD_DIM
KV_WIDTH = N_KV_HEADS * HEAD_DIM
WINDOW = 128
WINDOW_CHUNKS = WINDOW // CHUNK
D_CONV = 1024
CONV_K = 31
D_FF = 5632
FFN_CONV_K = 3
ROPE_THETA = 10000.0
EPS = 1e-6
NEG_INF = -1e30
N_IN = 2 * D_CONV + ATTN_WIDTH + 2 * KV_WIDTH + 2 * D_MODEL

kernel_name = "hybrid_streaming_conformer_swa_sink_step"


def _rms_norm(x, g):
    xf = x.astype(jnp.float32)
    y = xf * lax.rsqrt(jnp.mean(xf * xf, axis=-1, keepdims=True) + EPS)
    return (y * g.astype(jnp.float32)).astype(x.dtype)


def _layer_norm(x, g, b):
    xf = x.astype(jnp.float32)
    mu = jnp.mean(xf, axis=-1, keepdims=True)
    var = jnp.mean(jnp.square(xf - mu), axis=-1, keepdims=True)
    y = (xf - mu) * lax.rsqrt(var + EPS)
    return (y * g.astype(jnp.float32) + b.astype(jnp.float32)).astype(x.dtype)


def _causal_dwconv(x, ctx, w, b):
    xp = jnp.concatenate([ctx.astype(x.dtype), x], axis=1)
    y = lax.conv_general_dilated(xp, w.astype(x.dtype)[:, None, :], window_strides=(1,), padding='VALID',
                                 dimension_numbers=('NWC', 'WIO', 'NWC'), feature_group_count=x.shape[-1])
    k = w.shape[0]
    return y + b.astype(x.dtype), xp[:, xp.shape[1] - (k - 1):]


def _rope(x, pos):
    half = HEAD_DIM // 2
    inv_freq = 1.0 / (ROPE_THETA ** (jnp.arange(half, dtype=jnp.float32) / half))
    ang = pos.astype(jnp.float32)[:, None] * inv_freq[None, :]
    cos = jnp.cos(ang)[None, :, None, :]
    sin = jnp.sin(ang)[None, :, None, :]
    xf = x.astype(jnp.float32)
    x1, x2 = xf[..., :half], xf[..., half:]
    return jnp.concatenate([x1 * cos - x2 * sin, x2 * cos + x1 * sin], axis=-1).astype(x.dtype)


def _sink_attention(q, k, v, mask, sinks):
    s = jnp.einsum('bnqhgd,bnkhd->bnhgqk', q, k, preferred_element_type=jnp.float32) * (HEAD_DIM ** -0.5)
    s = jnp.where(mask[None, :, None, None], s, NEG_INF)
    sink = sinks.astype(jnp.float32).reshape(1, 1, N_KV_HEADS, GROUP, 1, 1)
    m = jnp.maximum(jnp.max(s, axis=-1, keepdims=True), sink)
    p = jnp.exp(s - m)
    denom = jnp.sum(p, axis=-1, keepdims=True) + jnp.exp(sink - m)
    return jnp.einsum('bnhgqk,bnkhd->bnqhgd', (p / denom).astype(v.dtype), v)


def _attend_prompt(q, k, v, sinks, win_rows):
    B, T = q.shape[0], q.shape[1]
    nc = T // CHUNK
    pad = WINDOW_CHUNKS * CHUNK
    qb = q.reshape(B, nc, CHUNK, N_KV_HEADS, GROUP, HEAD_DIM)
    zeros = jnp.zeros((B, pad, N_KV_HEADS, HEAD_DIM), k.dtype)
    kp = jnp.concatenate([zeros, k], axis=1).reshape(B, nc + WINDOW_CHUNKS, CHUNK, N_KV_HEADS, HEAD_DIM)
    vp = jnp.concatenate([zeros.astype(v.dtype), v], axis=1).reshape(B, nc + WINDOW_CHUNKS, CHUNK, N_KV_HEADS, HEAD_DIM)
    kb = jnp.concatenate([kp[:, j:j + nc] for j in range(WINDOW_CHUNKS + 1)], axis=2)
    vb = jnp.concatenate([vp[:, j:j + nc] for j in range(WINDOW_CHUNKS + 1)], axis=2)
    key_pos = jnp.arange(nc)[:, None] * CHUNK - pad + jnp.arange((WINDOW_CHUNKS + 1) * CHUNK)[None, :]
    mask = (key_pos >= 0)[:, None, :]
    o = _sink_attention(qb, kb, vb, mask, sinks).reshape(B, T, ATTN_WIDTH)
    return o, k[:, T - win_rows:], v[:, T - win_rows:]


def _attend_sample(q, k, v, k_cache, v_cache, sinks):
    B, T = q.shape[0], q.shape[1]
    w = k_cache.shape[1]
    kk = jnp.concatenate([k_cache.astype(k.dtype), k], axis=1)
    vv = jnp.concatenate([v_cache.astype(v.dtype), v], axis=1)
    qb = q.reshape(B, 1, T, N_KV_HEADS, GROUP, HEAD_DIM)
    mask = jnp.ones((1, 1, w + T), dtype=bool)
    o = _sink_attention(qb, kk[:, None], vv[:, None], mask, sinks).reshape(B, T, ATTN_WIDTH)
    return o, kk[:, T:], vv[:, T:]


def _layer(x, c, pos, conv_ctx, ffn_ctx, k_cache, v_cache, win_rows,
           mod_w, mod_b, norm1_g, w_in, b_in, conv_w, conv_b, ln_g, ln_b, conv_out_w,
           q_norm_g, k_norm_g, sinks, attn_o_w, w_out, norm2_g, ffn_up_w, ffn_conv_w, ffn_conv_b, ffn_down_w):
    B, T, _ = x.shape
    mod = jax.nn.silu(c) @ mod_w + mod_b
    sh1, sc1, g1, sh2, sc2, g2 = jnp.split(mod[:, None, :], 6, axis=-1)
    h = _rms_norm(x, norm1_g) * (1 + sc1) + sh1
    z = h @ w_in + b_in
    o1 = D_CONV
    o2 = o1 + D_CONV
    o3 = o2 + ATTN_WIDTH
    o4 = o3 + KV_WIDTH
    o5 = o4 + KV_WIDTH
    o6 = o5 + D_MODEL
    za, zb, zq, zk, zv, zgc, zga = jnp.split(z, [o1, o2, o3, o4, o5, o6], axis=-1)
    glu = za * jax.nn.sigmoid(zb)
    dw, new_conv = _causal_dwconv(glu, conv_ctx, conv_w, conv_b)
    y_conv = jax.nn.silu(_layer_norm(dw, ln_g, ln_b)) @ conv_out_w
    q = _rope(_rms_norm(zq.reshape(B, T, N_HEADS, HEAD_DIM), q_norm_g), pos)
    k = _rope(_rms_norm(zk.reshape(B, T, N_KV_HEADS, HEAD_DIM), k_norm_g), pos)
    v = zv.reshape(B, T, N_KV_HEADS, HEAD_DIM)
    if k_cache is None:
        o, new_k, new_v = _attend_prompt(q, k, v, sinks, win_rows)
    else:
        o, new_k, new_v = _attend_sample(q, k, v, k_cache, v_cache, sinks)
    y_attn = o @ attn_o_w
    merged = jax.nn.sigmoid(zgc) * y_conv + jax.nn.sigmoid(zga) * y_attn
    x = x + g1 * (merged @ w_out)
    h2 = _rms_norm(x, norm2_g) * (1 + sc2) + sh2
    up = h2 @ ffn_up_w
    ug, uv = jnp.split(up, 2, axis=-1)
    ugc, new_ffn = _causal_dwconv(ug, ffn_ctx, ffn_conv_w, ffn_conv_b)
    x = x + g2 * ((jax.nn.silu(ugc) * uv) @ ffn_down_w)
    return x, new_conv, new_k, new_v, new_ffn


def setup_inputs(seed: int = 0) -> dict:
    key = jax.random.key(seed)
    ks = jax.random.split(key, 32)
    f32 = jnp.float32
    win_rows = min(WINDOW, PAST_LEN)

    def nrm(k, shape, scale):
        return jax.random.normal(k, shape, f32) * scale

    return {
        "x_prompt": nrm(ks[0], (BATCH, SEQ, D_MODEL), 1.0),
        "x_sample": nrm(ks[1], (DEC_BATCH, DEC_SEQ, D_MODEL), 1.0),
        "c_prompt": nrm(ks[2], (BATCH, D_MODEL), 1.0),
        "c_sample": nrm(ks[3], (DEC_BATCH, D_MODEL), 1.0),
        "cache_conv": nrm(ks[4], (DEPTH, DEC_BATCH, CONV_K - 1, D_CONV), 1.0),
        "cache_k": nrm(ks[5], (DEPTH, DEC_BATCH, win_rows, N_KV_HEADS, HEAD_DIM), 1.0),
        "cache_v": nrm(ks[6], (DEPTH, DEC_BATCH, win_rows, N_KV_HEADS, HEAD_DIM), 1.0),
        "cache_ffn_conv": nrm(ks[7], (DEPTH, DEC_BATCH, FFN_CONV_K - 1, D_FF), 1.0),
        "mod_w": nrm(ks[8], (DEPTH, D_MODEL, 6 * D_MODEL), 0.2 * D_MODEL ** -0.5),
        "mod_b": nrm(ks[9], (DEPTH, 6 * D_MODEL), 0.01),
        "norm1_g": 1.0 + nrm(ks[10], (DEPTH, D_MODEL), 0.02),
        "w_in": nrm(ks[11], (DEPTH, D_MODEL, N_IN), D_MODEL ** -0.5),
        "b_in": nrm(ks[12], (DEPTH, N_IN), 0.01),
        "conv_w": nrm(ks[13], (DEPTH, CONV_K, D_CONV), CONV_K ** -0.5),
        "conv_b": nrm(ks[14], (DEPTH, D_CONV), 0.01),
        "ln_g": 1.0 + nrm(ks[15], (DEPTH, D_CONV), 0.02),
        "ln_b": nrm(ks[16], (DEPTH, D_CONV), 0.01),
        "conv_out_w": nrm(ks[17], (DEPTH, D_CONV, D_MODEL), D_CONV ** -0.5),
        "q_norm_g": 1.0 + nrm(ks[18], (DEPTH, HEAD_DIM), 0.02),
        "k_norm_g": 1.0 + nrm(ks[19], (DEPTH, HEAD_DIM), 0.02),
        "sinks": nrm(ks[20], (DEPTH, N_HEADS), 0.5),
        "attn_o_w": nrm(ks[21], (DEPTH, ATTN_WIDTH, D_MODEL), ATTN_WIDTH ** -0.5),
        "w_out": nrm(ks[22], (DEPTH, D_MODEL, D_MODEL), D_MODEL ** -0.5),
        "norm2_g": 1.0 + nrm(ks[23], (DEPTH, D_MODEL), 0.02),
        "ffn_up_w": nrm(ks[24], (DEPTH, D_MODEL, 2 * D_FF), D_MODEL ** -0.5),
        "ffn_conv_w": nrm(ks[25], (DEPTH, FFN_CONV_K, D_FF), FFN_CONV_K ** -0.5),
        "ffn_conv_b": nrm(ks[26], (DEPTH, D_FF), 0.01),
        "ffn_down_w": nrm(ks[27], (DEPTH, D_FF, D_MODEL), D_FF ** -0.5),
    }


def reference(x_prompt, x_sample, c_prompt, c_sample, cache_conv, cache_k, cache_v, cache_ffn_conv,
              mod_w, mod_b, norm1_g, w_in, b_in, conv_w, conv_b, ln_g, ln_b, conv_out_w,
              q_norm_g, k_norm_g, sinks, attn_o_w, w_out, norm2_g, ffn_up_w, ffn_conv_w, ffn_conv_b, ffn_down_w):
    win_rows = cache_k.shape[2]
    pos_p = jnp.arange(x_prompt.shape[1])
    pos_s = PAST_LEN + jnp.arange(x_sample.shape[1])
    bp = x_prompt.shape[0]
    yp, ys = x_prompt, x_sample
    conv_p, conv_s, k_p, k_s, v_p, v_s, ffn_p, ffn_s = [], [], [], [], [], [], [], []
    for l in range(DEPTH):
        w = (mod_w[l], mod_b[l], norm1_g[l], w_in[l], b_in[l], conv_w[l], conv_b[l], ln_g[l], ln_b[l],
             conv_out_w[l], q_norm_g[l], k_norm_g[l], sinks[l], attn_o_w[l], w_out[l], norm2_g[l],
             ffn_up_w[l], ffn_conv_w[l], ffn_conv_b[l], ffn_down_w[l])
        zc = jnp.zeros((bp, CONV_K - 1, D_CONV), yp.dtype)
        zf = jnp.zeros((bp, FFN_CONV_K - 1, D_FF), yp.dtype)
        yp, nc_p, nk_p, nv_p, nf_p = _layer(yp, c_prompt, pos_p, zc, zf, None, None, win_rows, *w)
        ys, nc_s, nk_s, nv_s, nf_s = _layer(ys, c_sample, pos_s, cache_conv[l], cache_ffn_conv[l],
                                            cache_k[l], cache_v[l], win_rows, *w)
        conv_p.append(nc_p); conv_s.append(nc_s)
        k_p.append(nk_p); k_s.append(nk_s)
        v_p.append(nv_p); v_s.append(nv_s)
        ffn_p.append(nf_p); ffn_s.append(nf_s)
    return (yp, ys, jnp.stack(conv_p), jnp.stack(conv_s), jnp.stack(k_p), jnp.stack(k_s),
            jnp.stack(v_p), jnp.stack(v_s), jnp.stack(ffn_p), jnp.stack(ffn_s))
```

```python
import numpy as np
from contextlib import ExitStack

import concourse.bass as bass
import concourse.mybir as mybir
from concourse.bass_utils import run_bass_kernel_spmd

F32 = mybir.dt.float32
BF16 = mybir.dt.bfloat16
AF = mybir.ActivationFunctionType
ALU = mybir.AluOpType

N_CORES = 8
EPS = 1e-6


class Cfg:
    D = 2048
    DC = 1024
    CK = 31
    NH = 16
    NKV = 4
    HD = 64
    DFF = 5632
    FK = 3
    SEQ = 2048
    DEC = 64
    PAST = 1024
    WIN = 128
    NSEQ = 2
    NSMP = 2
    T = 512
    THETA = 10000.0


BLK = 512
PSBANK = 2048


class Rec:
    __slots__ = ("eng", "emit", "stream", "ordinal", "deps", "waits", "inc", "incval", "vc", "isdma", "newgrp")


class Prog:
    ENG_ATTR = {"pe": "tensor", "act": "scalar", "dve": "vector", "pool": "gpsimd", "sp": "sync"}

    def __init__(self, nc):
        self.nc = nc
        self.recs = []
        self.count = {}
        self.lastw = {}
        self.readers = {}
        self.tinfo = {}
        self.tracked_dram = set()

    def sb(self, name, shape, dtype, offset):
        t = self.nc.alloc_sbuf_tensor_at(name, list(shape), dtype, offset=offset)
        es = 4 if dtype == F32 else 2
        self.tinfo[t.name] = ("sb", offset, es)
        return t

    def region(self, ap):
        name = ap.tensor.name
        info = self.tinfo.get(name)
        if info is None:
            if name in self.tracked_dram:
                return [("d:" + name, 0)]
            return []
        space, base, es = info
        pat = ap.ap
        pstep = pat[0][0]
        off = ap.offset % pstep if pstep > 0 else ap.offset
        gran = PSBANK if space == "ps" else BLK
        dims = sorted([(abs(st) * es, n) for st, n in pat[1:] if n > 1 and st != 0], reverse=True)
        out = set()

        def rec(lo, ds):
            ext = es
            for st, n in ds:
                ext += st * (n - 1)
            if ds and ds[0][0] >= gran and ds[0][1] <= 512:
                inner = es
                for st, n in ds[1:]:
                    inner += st * (n - 1)
                if ds[0][0] >= inner:
                    for i in range(ds[0][1]):
                        rec(lo + i * ds[0][0], ds[1:])
                    return
            for b_ in range(lo // gran, (lo + ext - 1) // gran + 1):
                out.add(b_)
        rec(base + off * es, dims)
        return [(space, b_) for b_ in sorted(out)]

    def op(self, eng, emit, reads=(), writes=(), key=None, chain=True):
        r = Rec()
        r.eng = eng
        r.emit = emit
        r.isdma = key is not None
        r.stream = ("dma:" + key) if key is not None else eng
        r.ordinal = self.count.get(r.stream, 0)
        self.count[r.stream] = r.ordinal + 1
        r.newgrp = bool(chain) or r.ordinal == 0
        deps = {}
        if r.isdma and r.newgrp and r.ordinal > 0:
            deps[r.stream] = r.ordinal - 1

        def add(s, o, e, raw):
            if s == eng and not r.isdma:
                if eng == "pe" or not raw:
                    return
            if r.isdma and s == r.stream:
                return
            if s not in deps or deps[s] < o:
                deps[s] = o

        rblocks = []
        for ap in reads:
            rblocks += self.region(ap)
        wblocks = []
        for ap in writes:
            wblocks += self.region(ap)
        for b in rblocks:
            w = self.lastw.get(b)
            if w is not None:
                add(w[0], w[1], w[2], True)
        for b in wblocks:
            w = self.lastw.get(b)
            if w is not None:
                add(w[0], w[1], w[2], False)
            rd = self.readers.get(b)
            if rd:
                for s, (o, e) in rd.items():
                    add(s, o, e, False)
        me = (r.stream, r.ordinal, eng)
        for b in rblocks:
            self.readers.setdefault(b, {})[r.stream] = (r.ordinal, eng)
        for b in wblocks:
            self.lastw[b] = me
            self.readers[b] = {}
        r.deps = deps
        self.recs.append(r)
        return r

    def finalize(self, final_wait_eng="pool"):
        nc = self.nc
        known = {e: {} for e in self.ENG_ATTR}
        needed = {}
        bystream = {}
        for r in self.recs:
            bystream.setdefault(r.stream, []).append(r)
        grp_end = {}
        for s, lst in bystream.items():
            if s.startswith("dma:"):
                ge = [0] * len(lst)
                end = len(lst) - 1
                for i in range(len(lst) - 1, -1, -1):
                    ge[i] = end
                    if lst[i].newgrp:
                        end = i - 1
                grp_end[s] = ge
        for r in self.recs:
            kn = known[r.eng]
            waits = []
            for s, o in r.deps.items():
                if s in grp_end:
                    o = grp_end[s][o]
                    assert not (s == r.stream and o >= r.ordinal)
                if kn.get(s, -1) >= o:
                    continue
                waits.append((s, o))
            for s, o in waits:
                needed.setdefault(s, set()).add(o)
                dep = bystream[s][o]
                if kn.get(s, -1) < o:
                    kn[s] = o
                for s2, o2 in dep.vc.items():
                    if kn.get(s2, -1) < o2:
                        kn[s2] = o2
            r.waits = waits
            r.vc = dict(kn)
        final = []
        for s, lst in bystream.items():
            if s.startswith("dma:"):
                needed[s] = set(range(len(lst)))
                final.append((s, len(lst) - 1))
        cnt = {}
        for s, lst in bystream.items():
            need = needed.get(s, set())
            c = 0
            tab = {}
            for r in lst:
                if r.ordinal in need:
                    c += 16 if r.isdma else 1
                    r.inc = True
                    tab[r.ordinal] = c
                else:
                    r.inc = False
            cnt[s] = tab
        self.nsem = len(bystream)
        with ExitStack() as es:
            sems = {}
            for i, s in enumerate(sorted(bystream)):
                sems[s] = es.enter_context(nc.semaphore("s%d" % i))
            block = es.enter_context(nc.Block())
            per_eng = {e: [] for e in self.ENG_ATTR}
            for r in self.recs:
                per_eng[r.eng].append(r)
            for e, attr in self.ENG_ATTR.items():
                lst = per_eng[e]
                fin = final if e == final_wait_eng else []

                def body(h, lst=lst, fin=fin):
                    for r in lst:
                        for s, o in r.waits:
                            h.wait_ge(sems[s], cnt[s][o])
                        ins = r.emit(h)
                        if r.inc:
                            ins.then_inc(sems[r.stream], 16 if r.isdma else 1)
                    for s, o in fin:
                        h.wait_ge(sems[s], cnt[s][o])
                if lst or fin:
                    getattr(block, attr)(body)


class Ops:
    def __init__(self, P):
        self.P = P

    def mm(self, out, lhsT, rhs, start=True, stop=True):
        return self.P.op("pe", lambda h: h.matmul(out, lhsT=lhsT, rhs=rhs, start=start, stop=stop),
                         reads=[lhsT, rhs], writes=[out])

    def tr(self, out, in_, ident):
        return self.P.op("pe", lambda h: h.transpose(out, in_, ident), reads=[in_, ident], writes=[out])

    def act(self, out, in_, func, bias=0.0, scale=1.0, accum_out=None):
        rd = [in_]
        if not isinstance(bias, (int, float)):
            rd.append(bias)
        if not isinstance(scale, (int, float)):
            rd.append(scale)
        wr = [out]
        if accum_out is not None:
            wr.append(accum_out)
        kw = {}
        if accum_out is not None:
            kw["accum_out"] = accum_out
        return self.P.op("act", lambda h: h.activation(out, in_, func, bias=bias, scale=scale, **kw),
                         reads=rd, writes=wr)

    def tt(self, out, in0, in1, op, eng="dve"):
        return self.P.op(eng, lambda h: h.tensor_tensor(out, in0, in1, op), reads=[in0, in1], writes=[out])

    def ts(self, out, in0, s1, s2, op0, op1=None, eng="dve"):
        rd = [in0]
        if not isinstance(s1, (int, float)):
            rd.append(s1)
        if s2 is not None and not isinstance(s2, (int, float)):
            rd.append(s2)
        if op1 is None:
            return self.P.op(eng, lambda h: h.tensor_scalar(out, in0, s1, None, op0), reads=rd, writes=[out])
        return self.P.op(eng, lambda h: h.tensor_scalar(out, in0, s1, s2, op0, op1), reads=rd, writes=[out])

    def stt(self, out, in0, scalar, in1, op0, op1):
        rd = [in0, in1]
        if not isinstance(scalar, (int, float)):
            rd.append(scalar)
        return self.P.op("dve", lambda h: h.scalar_tensor_tensor(out, in0, scalar, in1, op0, op1),
                         reads=rd, writes=[out])

    def copy(self, out, in_, eng="dve"):
        if eng == "act":
            return self.act(out, in_, AF.Copy)
        return self.P.op(eng, lambda h: h.tensor_copy(out, in_), reads=[in_], writes=[out])

    def memset(self, out, val, eng="pool"):
        return self.P.op(eng, lambda h: h.memset(out, val), writes=[out])

    def recip(self, out, in_):
        return self.P.op("dve", lambda h: h.reciprocal(out, in_), reads=[in_], writes=[out])

    def dma(self, eng, out, in_, key, chain=True):
        return self.P.op(eng, lambda h: h.dma_start(out=out, in_=in_), reads=[in_], writes=[out], key=key, chain=chain)


def ceil_to(x, a):
    return (x + a - 1) // a * a


class Arena:
    def __init__(self, base, size):
        self.base, self.size, self.cur = base, size, base

    def reset(self):
        self.cur = self.base

    def take(self, nbytes, align=BLK):
        off = ceil_to(self.cur, align)
        self.cur = off + nbytes
        assert self.cur <= self.base + self.size, ("arena overflow", self.cur - self.base, self.size)
        return off


def build_program(C):
    nc = bass.Bass("TRN2", target_bir_lowering=False)
    P = Prog(nc)
    O = Ops(P)

    D, DC, CK, NH, NKV, HD, DFF, FK = C.D, C.DC, C.CK, C.NH, C.NKV, C.HD, C.DFF, C.FK
    SEQ, DEC, NSEQ, NSMP, T = C.SEQ, C.DEC, C.NSEQ, C.NSMP, C.T
    assert HD == 64 and NH // NKV == 4 and DEC == 64 and NSMP == 2
    FC = D // 128
    CC = DC // 128
    AW = NH * HD
    KVW = NKV * HD
    QC = AW // 128
    KC2 = KVW // 128
    FFC = DFF // 128
    NIN = 2 * DC + AW + 2 * KVW + 2 * D
    NSQ = NSEQ + NSMP
    HK = CK - 1
    TSM = DEC * NSMP
    NT = SEQ // T
    o_zb, o_q, o_k, o_v, o_gc, o_ga = DC, 2 * DC, 2 * DC + AW, 2 * DC + AW + KVW, 2 * DC + AW + 2 * KVW, 2 * DC + AW + 2 * KVW + D
    assert DC % 256 == 0 and AW % 512 == 0 and D % 512 == 0 and DFF % 256 == 0 and KVW * 2 <= 512

    def din(name, shape, dt=F32):
        return nc.dram_tensor(name, list(shape), dt, kind="ExternalInput").ap()

    def dout(name, shape):
        return nc.dram_tensor(name, list(shape), F32, kind="ExternalOutput").ap()

    def dscr(name, shape, dt=BF16):
        t = nc.dram_tensor(name, list(shape), dt, kind="Internal")
        P.tracked_dram.add(t.name)
        return t.ap()

    xp = din("xp", [NSEQ, SEQ, D])
    xs = din("xs", [NSMP * DEC, D])
    call = din("call", [NSQ, D])
    cconv = din("cconv", [NSMP, HK, DC])
    ck_in = din("ck", [NSMP, C.WIN, KVW])
    cv_in = din("cv", [NSMP, C.WIN, KVW])
    cffn = din("cffn", [NSMP, FK - 1, DFF])
    mod_w = din("mod_w", [D, 6 * D])
    mod_b = din("mod_b", [6 * D])
    norm1_g = din("norm1_g", [D])
    w_in = din("w_in", [D, NIN])
    b_in = din("b_in", [NIN])
    conv_w = din("conv_w", [CK, DC])
    conv_b = din("conv_b", [DC])
    ln_g = din("ln_g", [DC])
    ln_b = din("ln_b", [DC])
    conv_out_w = din("conv_out_w", [DC, D])
    q_norm_g = din("q_norm_g", [HD])
    k_norm_g = din("k_norm_g", [HD])
    sinks = din("sinks", [NH])
    attn_o_w = din("attn_o_w", [AW, D])
    w_out = din("w_out", [D, D])
    norm2_g = din("norm2_g", [D])
    ffn_up_w = din("ffn_up_w", [D, 2 * DFF])
    ffn_conv_w = din("ffn_conv_w", [FK, DFF])
    ffn_conv_b = din("ffn_conv_b", [DFF])
    ffn_down_w = din("ffn_down_w", [DFF, D])
    NCST = 5 * 128
    cst_in = din("cst", [128, NCST])
    rope_in = din("rope", [2, 128, SEQ + DEC])

    yp = dout("yp", [NSEQ, SEQ, D])
    ys = dout("ys", [NSMP * DEC, D])
    ncv_p = dout("ncv_p", [NSEQ, HK, DC])
    ncv_s = dout("ncv_s", [NSMP, HK, DC])
    nk_p = dout("nk_p", [NSEQ, C.WIN, KVW])
    nk_s = dout("nk_s", [NSMP, C.WIN, KVW])
    nv_p = dout("nv_p", [NSEQ, C.WIN, KVW])
    nv_s = dout("nv_s", [NSMP, C.WIN, KVW])
    nf_p = dout("nf_p", [NSEQ, FK - 1, DFF])
    nf_s = dout("nf_s", [NSMP, FK - 1, DFF])


    SB0 = nc.sbuf_base
    SBTOP = nc.sbuf_top
    fixed = Arena(ceil_to(SB0, BLK), 26 * 1024)
    SLOT = 16 * 1024
    NSLOT = 3
    ringA = Arena(fixed.base + fixed.size, SLOT * NSLOT)
    H1 = Arena(ringA.base + ringA.size, 16 * 1024)
    H2 = Arena(H1.base + H1.size, 16 * 1024)
    X = Arena(H2.base + H2.size, 32 * 1024)
    Y = Arena(X.base + X.size, 52 * 1024)
    TM = Arena(Y.base + Y.size, SBTOP - (Y.base + Y.size))
    assert TM.size >= 17 * 1024, TM.size

    _n = [0]

    def sbt(arena, shape, dt, name=None):
        nbytes = int(np.prod(shape[1:])) * (4 if dt == F32 else 2)
        off = arena.take(nbytes)
        _n[0] += 1
        return P.sb("%s_%d" % (name or "t", _n[0]), shape, dt, off)

    pst = nc.alloc_psum_tensor("ps", [128, 8, 512], F32)
    P.tinfo[pst.name] = ("ps", 0, 4)
    _bank = [0]

    def bank():
        b = _bank[0] % 8
        _bank[0] += 1
        return b

    cst_f = sbt(fixed, [128, NCST], F32, "cstf")
    cst_b = sbt(fixed, [128, NCST], BF16, "cstb")
    ident_f = cst_f[:, 0:128]
    rotT_f = cst_f[:, 128:256]
    onesblk_b = cst_b[:, 256:384]
    dup_b = [cst_b[:, 384:512], cst_b[:, 512:640]]
    ones_b = sbt(fixed, [128, 128], BF16, "ones")
    oneln_b = sbt(fixed, [128, 128], BF16, "oneln")
    oneln_f = sbt(fixed, [128, 128], F32, "onelnf")
    ident_b = cst_b[:, 0:128]

    vrows = {}
    nrow = [0]

    def vreg(name, n):
        vrows[name] = nrow[0]
        nrow[0] += n
    vreg("b_in", NIN // 128)
    vreg("conv_w", CK * CC)
    vreg("conv_b", CC)
    vreg("ln_g", CC)
    vreg("ln_b", CC)
    vreg("norm1_g", FC)
    vreg("norm2_g", FC)
    vreg("ffn_conv_w", FK * FFC)
    vreg("ffn_conv_b", FFC)
    vreg("mod_b", 6 * FC)
    NVT = (nrow[0] + 127) // 128
    vecT = sbt(fixed, [128, NVT * 128], F32, "vecT")

    def vcol(name, i):
        j = vrows[name] + i
        return vecT[:, j:j + 1]

    modT = sbt(fixed, [128, 6 * FC, NSQ], F32, "modT")
    A1T = sbt(fixed, [128, FC, NSQ], F32, "A1T")
    A2T = sbt(fixed, [128, FC, NSQ], F32, "A2T")
    cT = sbt(fixed, [128, FC, NSQ], BF16, "cT")
    qkg = sbt(fixed, [128, 2], F32, "qkg")
    sinkexp = sbt(fixed, [128, NKV, 2, 64], F32, "sinkexp")
    se16 = sbt(fixed, [128, NH], F32, "se16")
    bvb = sbt(fixed, [64, KVW], F32, "bvb")
    gluH = sbt(fixed, [128, 2, CC, HK], F32, "gluH")
    gluHb = sbt(fixed, [128, 2, CC, HK], BF16, "gluHb")
    kH = sbt(fixed, [128, NKV, 128], BF16, "kH")
    vH = sbt(fixed, [64, 2, KVW], BF16, "vH")
    ffnH = sbt(fixed, [128, 2, FFC, FK - 1], F32, "ffnH")
    cosT = sbt(fixed, [128, T], F32, "cosT")
    sinT = sbt(fixed, [128, T], F32, "sinT")
    ssq = sbt(fixed, [128, 32], F32, "ssq")
    rstd_s = sbt(fixed, [128, 8], F32, "rstds")
    gbc = P.sb("gbc", [128, D], F32, Y.base + 44 * 1024)

    ring = [P.sb("ring%d" % i, [128, SLOT // 2], BF16, ringA.base + i * SLOT) for i in range(NSLOT)]

    def slab_pieces(wb, k0, nk, colranges):
        pcs = []
        c = 0
        for (c0, w) in colranges:
            pcs.append((wb[k0 * 128:(k0 + nk) * 128, c0:c0 + w], c, w))
            c += w
        return (nk, c, pcs)

    NMOD0 = 2 * FC // 4
    nmslab = 6 * D // 512

    def tile_slabs(t_index):
        L = []
        for s in range(DC // 256):
            L.append(("A", s, slab_pieces(w_in, 0, FC, [(256 * s, 256), (o_zb + 256 * s, 256)])))
        for s in range(AW // 512):
            L.append(("Q", s, slab_pieces(w_in, 0, FC, [(o_q + 512 * s, 512)])))
        L.append(("KV", 0, slab_pieces(w_in, 0, FC, [(o_k, 2 * KVW)])))
        if t_index == 0:
            for ms in range(NMOD0, nmslab):
                L.append(("MOD", ms, slab_pieces(mod_w, 0, FC, [(512 * ms, 512)])))
        for g in range(D // 512):
            L.append(("GC", g, slab_pieces(w_in, 0, FC, [(o_gc + 512 * g, 512)])))
            L.append(("CO", g, slab_pieces(conv_out_w, 0, CC, [(512 * g, 512)])))
            L.append(("GA", g, slab_pieces(w_in, 0, FC, [(o_ga + 512 * g, 512)])))
            L.append(("AO", g, slab_pieces(attn_o_w, 0, QC, [(512 * g, 512)])))
        for g in range(D // 512):
            L.append(("WO", g, slab_pieces(w_out, 0, FC, [(512 * g, 512)])))
        for s in range(DFF // 256):
            L.append(("UP", s, slab_pieces(ffn_up_w, 0, FC, [(256 * s, 256), (DFF + 256 * s, 256)])))
        for g in range(D // 512):
            k0 = 0
            while k0 < FFC:
                nk = min(16, FFC - k0)
                L.append(("DN", (g, k0), slab_pieces(ffn_down_w, k0, nk, [(512 * g, 512)])))
                k0 += nk
        return L

    n_tiles_total = NSEQ * NT + 1
    slabs = []
    scr = {}
    for t_ in range(n_tiles_total):
        for (kind, idx, desc) in tile_slabs(t_):
            if kind != "MOD" and (kind, idx) not in scr:
                scr[(kind, idx)] = dscr("scr%d" % len(scr), [128, desc[0] * desc[1]])
            slabs.append((kind, idx, desc, t_))

    class Ring:
        def __init__(self):
            self.issued = 0
            self.cons = 0

        def issue(self):
            if self.issued >= len(slabs):
                return
            i = self.issued
            kind, idx, (nk, W, pcs), t_ = slabs[i]
            slot = ring[i % NSLOT]
            v = slot[:, 0:nk * W].rearrange("p (k w) -> p k w", w=W)
            if kind == "UP":
                want = 2 + (idx % 2)
            elif kind == "DN":
                want = 4 + ((idx[0] * 3 + idx[1] // 16) % 2)
            else:
                want = 1
            store_t = max(0, min(want, n_tiles_total - 2))
            if t_ <= store_t:
                for pi, (src, c, w) in enumerate(pcs):
                    O.dma("pool", v[:, :, c:c + w], src.rearrange("(k p) w -> p k w", p=128), key="ring%d" % (i % NSLOT), chain=(pi == 0))
                if kind != "MOD" and t_ == store_t:
                    O.dma("sp", scr[(kind, idx)][:, :], slot[:, 0:nk * W], key="wst%d" % (i % NSLOT))
            else:
                O.dma("sp", slot[:, 0:nk * W], scr[(kind, idx)][:, :], key="ring%d" % (i % NSLOT))
            self.issued += 1

        def get(self, kind, idx):
            i = self.cons
            k2, i2, (nk, W, pcs), t_ = slabs[i]
            assert (k2, i2) == (kind, idx), (k2, i2, kind, idx)
            slot = ring[i % NSLOT]
            self.cons += 1
            return slot[:, 0:nk * W].rearrange("p (k w) -> p k w", w=W)

        def done(self):
            self.issue()

        def consume_mod(self, n=1, b=None):
            for _ in range(n):
                if self.cons < len(slabs) and slabs[self.cons][0] == "MOD":
                    ms = slabs[self.cons][1]
                    wv_ = self.get("MOD", ms)
                    mod_slab(ms, wv_, b)
                    self.issue()

    R = Ring()

    O.dma("sp", cst_f[:, :], cst_in[:, :], key="cst")
    O.copy(cst_b[:, :], cst_f[:, :], eng="dve")
    O.memset(ones_b[:, :], 1.0, eng="dve")
    O.memset(oneln_b[:, :], 1.0 / DC, eng="dve")
    O.memset(oneln_f[:, :], 1.0 / DC, eng="dve")

    TM.reset()
    stage = sbt(TM, [128, NVT, 128], F32, "vstage")
    O.memset(stage[:, :, :], 0.0, eng="dve")

    def stage_rows(name, src2d, n):
        r0 = vrows[name]
        done = 0
        while done < n:
            r = r0 + done
            t_, p_ = r // 128, r % 128
            m = min(n - done, 128 - p_)
            O.dma("sp", stage[p_:p_ + m, t_, :], src2d[done:done + m, :], key="vstage%d" % t_, chain=False)
            done += m
    stage_rows("b_in", b_in.rearrange("(c p) -> c p", p=128), NIN // 128)
    stage_rows("conv_w", conv_w.rearrange("k (c p) -> (k c) p", p=128), CK * CC)
    stage_rows("conv_b", conv_b.rearrange("(c p) -> c p", p=128), CC)
    stage_rows("ln_g", ln_g.rearrange("(c p) -> c p", p=128), CC)
    stage_rows("ln_b", ln_b.rearrange("(c p) -> c p", p=128), CC)
    stage_rows("norm1_g", norm1_g.rearrange("(c p) -> c p", p=128), FC)
    stage_rows("norm2_g", norm2_g.rearrange("(c p) -> c p", p=128), FC)
    stage_rows("ffn_conv_w", ffn_conv_w.rearrange("k (c p) -> (k c) p", p=128), FK * FFC)
    stage_rows("ffn_conv_b", ffn_conv_b.rearrange("(c p) -> c p", p=128), FFC)
    stage_rows("mod_b", mod_b.rearrange("(c p) -> c p", p=128), 6 * FC)
    for t_ in range(NVT):
        b = bank()
        O.tr(pst[:, b, 0:128], stage[:, t_, :], ident_f)
        O.act(vecT[:, t_ * 128:(t_ + 1) * 128], pst[:, b, 0:128], AF.Copy)

    cs = sbt(TM, [NSQ, D], F32, "cs")
    O.dma("sp", cs[:, :], call[:, :], key="cs")
    O.act(cs[:, :], cs[:, :], AF.Silu)
    b = bank()
    for fc in range(FC):
        O.tr(pst[:, b, fc * NSQ:(fc + 1) * NSQ], cs[:, fc * 128:(fc + 1) * 128], ident_f[0:NSQ, 0:NSQ])
    O.copy(cT[:, :, :], pst[:, b, 0:FC * NSQ].rearrange("p (f s) -> p f s", s=NSQ), eng="dve")

    O.dma("sp", qkg[0:64, 0:1], q_norm_g.rearrange("(p o) -> p o", o=1), key="qkg", chain=False)
    O.dma("sp", qkg[64:128, 0:1], q_norm_g.rearrange("(p o) -> p o", o=1), key="qkg", chain=False)
    O.dma("sp", qkg[0:64, 1:2], k_norm_g.rearrange("(p o) -> p o", o=1), key="qkg", chain=False)
    O.dma("sp", qkg[64:128, 1:2], k_norm_g.rearrange("(p o) -> p o", o=1), key="qkg", chain=False)
    O.dma("sp", se16[:, :], sinks.partition_broadcast(128), key="se16")
    O.dma("sp", bvb[:, :], b_in[o_v:o_v + KVW].partition_broadcast(64), key="bvb")
    O.act(se16[:, :], se16[:, :], AF.Exp)
    sev = se16[:, :].rearrange("p (g j h) -> p g j h", j=2, h=2)
    for half in range(2):
        O.copy(sinkexp[half * 64:(half + 1) * 64, :, :, :],
               sev[half * 64:(half + 1) * 64, :, :, half:half + 1].to_broadcast([64, NKV, 2, 64]), eng="dve")


    tiles = []
    for s in range(NSEQ):
        for ti in range(NT):
            tiles.append(dict(kind="p", s=s, ti=ti, T=T, L=T, nseg=1, segs=[s], first=(ti == 0), last=(ti == NT - 1)))
    tiles.append(dict(kind="s", T=TSM, L=DEC, nseg=NSMP, segs=[NSEQ + i for i in range(NSMP)], first=False, last=True))

    def x_src(tl):
        if tl["kind"] == "p":
            return xp[tl["s"], tl["ti"] * T:(tl["ti"] + 1) * T, :].rearrange("(tb p) d -> p tb d", p=128)
        return xs[:, :].rearrange("(tb p) d -> p tb d", p=128)

    def y_dst(tl):
        if tl["kind"] == "p":
            return yp[tl["s"], tl["ti"] * T:(tl["ti"] + 1) * T, :].rearrange("(tb p) d -> p tb d", p=128)
        return ys[:, :].rearrange("(tb p) d -> p tb d", p=128)

    def x_tensor(tl, arena_base, name):
        TB = tl["T"] // 128
        return P.sb(name, [128, TB, D], F32, arena_base)

    xt0 = x_tensor(tiles[0], X.base, "xt_0")
    O.dma("pool", xt0[:, :, :], x_src(tiles[0]), key="xload")

    def mod_slab(ms, v, b=None):
        if b is None:
            b = bank()
        for i in range(4):
            for kc in range(FC):
                O.mm(pst[:, b, i * NSQ:(i + 1) * NSQ], lhsT=v[:, kc, i * 128:(i + 1) * 128], rhs=cT[:, kc, :],
                     start=(kc == 0), stop=(kc == FC - 1))
        m0 = ms * 4
        O.tt(modT[:, m0:m0 + 4, :], pst[:, b, 0:4 * NSQ].rearrange("p (m s) -> p m s", s=NSQ),
             vecT[:, vrows["mod_b"] + m0: vrows["mod_b"] + m0 + 4].unsqueeze(2).to_broadcast([128, 4, NSQ]), ALU.add)

    for ms in range(NMOD0):
        slot = ring[ms % NSLOT]
        v = slot[:, 0:FC * 512].rearrange("p (k w) -> p k w", w=512)
        O.dma("pool", v, mod_w[:, ms * 512:(ms + 1) * 512].rearrange("(k p) w -> p k w", p=128), key="ring%d" % (ms % NSLOT))
        mod_slab(ms, v)
    sh1 = modT[:, 0 * FC:1 * FC, :]
    sc1 = modT[:, 1 * FC:2 * FC, :]
    g1T = modT[:, 2 * FC:3 * FC, :]
    sh2 = modT[:, 3 * FC:4 * FC, :]
    sc2 = modT[:, 4 * FC:5 * FC, :]
    g2T = modT[:, 5 * FC:6 * FC, :]
    n1g = vecT[:, vrows["norm1_g"]:vrows["norm1_g"] + FC].unsqueeze(2).to_broadcast([128, FC, NSQ])
    n2g = vecT[:, vrows["norm2_g"]:vrows["norm2_g"] + FC].unsqueeze(2).to_broadcast([128, FC, NSQ])
    O.ts(A1T[:, :, :], sc1, 1.0, None, ALU.add)
    O.tt(A1T[:, :, :], A1T[:, :, :], n1g, ALU.mult)

    if getattr(C, "DEBUG", False):
        dbg_mod = dout("dbg_mod", [128, 6 * FC * NSQ])
        dbg_vec = dout("dbg_vec", [128, NVT * 128])
        dbg_ct = dout("dbg_ct", [128, FC * NSQ])
        ctf = sbt(TM, [128, FC * NSQ], F32, "ctf")
        O.copy(ctf[:, :], cT[:, :, :].rearrange("p f s -> p (f s)"), eng="dve")
        O.dma("pool", dbg_mod[:, :], modT[:, :, :].rearrange("p m s -> p (m s)"), key="dbg")
        O.dma("pool", dbg_vec[:, :], vecT[:, :], key="dbg")
        O.dma("pool", dbg_ct[:, :], ctf[:, :], key="dbg")

    for _ in range(NSLOT):
        R.issue()

    def seg_cols(tl, si):
        L = tl["L"]
        return slice(si * L, (si + 1) * L)

    def rsqrt_act(out, in_, scale, n):
        O.act(out, in_, AF.Ln, bias=EPS, scale=scale)
        O.act(out, out, AF.Exp, scale=-0.5)

    def norm_phase(tl, xt, xn, hT_dst, AT, shT, ssq_done=False):
        Tt, TB = tl["T"], tl["T"] // 128
        junk = sbt(TM, [128, D], BF16, "junk")
        if not ssq_done:
            for tb in range(TB):
                O.act(junk[:, :], xt[:, tb, :], AF.Square, accum_out=ssq[:, tb:tb + 1])
        rsqrt_act(rstd_s[:, 0:TB], ssq[:, 0:TB], 1.0 / D, 128)
        for tb in range(TB):
            O.ts(xn[:, tb, :], xt[:, tb, :], rstd_s[:, tb:tb + 1], None, ALU.mult)
        for fc in range(FC):
            b = bank()
            for tb in range(TB):
                O.tr(pst[:, b, tb * 128:(tb + 1) * 128], xn[:, tb, fc * 128:(fc + 1) * 128], ident_f)
            for si, sq in enumerate(tl["segs"]):
                cs_ = seg_cols(tl, si)
                if fc % 2 == 0:
                    O.act(hT_dst[:, fc, cs_], pst[:, b, cs_], AF.Identity,
                          bias=shT[:, fc, sq:sq + 1], scale=AT[:, fc, sq:sq + 1])
                else:
                    O.ts(hT_dst[:, fc, cs_], pst[:, b, cs_], AT[:, fc, sq:sq + 1], shT[:, fc, sq:sq + 1], ALU.mult, ALU.add)

    def build_gbc(tl, gT):
        Tt = tl["T"]
        _n[0] += 1
        bcin = P.sb("bcin_%d" % _n[0], [128, 4, 128], F32, Y.base + 42 * 1024)
        for f4 in range(FC // 4):
            b = bank()
            for i in range(4):
                fc = f4 * 4 + i
                for si, sq in enumerate(tl["segs"]):
                    w = 128 // tl["nseg"]
                    O.copy(bcin[:, i, si * w:(si + 1) * w], gT[:, fc, sq:sq + 1].to_broadcast([128, w]), eng="dve")
                O.tr(pst[:, b, i * 128:(i + 1) * 128], bcin[:, i, :], ident_f)
            O.act(gbc[:, f4 * 512:(f4 + 1) * 512], pst[:, b, :], AF.Copy)

    def norm1_prefetch_gen(ntl, xsrc, hT_next):
        nTB = ntl["T"] // 128
        G_ = D // 512
        pv_ = ssq[:, 8:8 + nTB * G_].rearrange("p (t g) -> p t g", g=G_)
        if G_ == 1:
            O.copy(ssq[:, 0:nTB], pv_[:, :, 0], eng="dve")
        else:
            O.tt(ssq[:, 0:nTB], pv_[:, :, 0], pv_[:, :, 1], ALU.add)
            for g_ in range(2, G_):
                O.tt(ssq[:, 0:nTB], ssq[:, 0:nTB], pv_[:, :, g_], ALU.add)
        rsqrt_act(rstd_s[:, 0:nTB], ssq[:, 0:nTB], 1.0 / D, 128)
        for tb in range(nTB):
            for g_ in range(G_):
                O.ts(xsrc[g_][:, tb, :], xsrc[g_][:, tb, :], rstd_s[:, tb:tb + 1], None, ALU.mult)
        yield
        fb = [4, 5, 6, 7]
        for fc in range(FC):
            b = fb[fc % 4]
            g_, c_ = (fc * 128) // 512, (fc * 128) % 512
            for tb in range(nTB):
                O.tr(pst[:, b, tb * 128:(tb + 1) * 128], xsrc[g_][:, tb, c_:c_ + 128], ident_f)
            for si, sq in enumerate(ntl["segs"]):
                cs_ = slice(si * ntl["L"], (si + 1) * ntl["L"])
                if fc % 2 == 0:
                    O.act(hT_next[:, fc, cs_], pst[:, b, cs_], AF.Identity, bias=sh1[:, fc, sq:sq + 1], scale=A1T[:, fc, sq:sq + 1])
                else:
                    O.ts(hT_next[:, fc, cs_], pst[:, b, cs_], A1T[:, fc, sq:sq + 1], sh1[:, fc, sq:sq + 1], ALU.mult, ALU.add)
            yield

    Hbufs = [H1, H2]
    nxt_xt = xt0
    for tix, tl in enumerate(tiles):
        Tt, L, nseg, TB = tl["T"], tl["L"], tl["nseg"], tl["T"] // 128
        NCH = L // 64
        xt = nxt_xt
        hT = P.sb("hT_%d" % tix, [128, FC, Tt], BF16, Hbufs[0].base)
        mT = P.sb("mT_%d" % tix, [128, FC, Tt], BF16, Hbufs[1].base)
        h2T = hT

        use_tm = (Tt == 128)
        tm_cache = {}
        pend = {}
        tm_stage = [P.sb("tmst%d_%d" % (i, tix), [128, 512], F32, Y.base + 36 * 1024 + i * 2048) for i in range(2)]
        tm_n = [0]

        def tm_prefetch(wv_, act, nk):
            W_ = wv_.shape[2]
            bt = bank()
            for kc in range(nk):
                O.mm(pst[:, bt, 0:W_], lhsT=act[:, kc, :], rhs=wv_[:, kc, :], start=(kc == 0), stop=(kc == nk - 1))
            stg = tm_stage[tm_n[0] % 2]
            tm_n[0] += 1
            if tm_n[0] % 2:
                O.act(stg[:, 0:W_], pst[:, bt, 0:W_], AF.Copy)
            else:
                O.copy(stg[:, 0:W_], pst[:, bt, 0:W_], eng="dve")
            tm_cache[id(wv_)] = (stg, wv_)

        def pre(kind, idx, act, nk):
            if use_tm:
                wv_ = R.get(kind, idx)
                tm_prefetch(wv_, act, nk)
                pend[(kind, idx)] = wv_

        def getp(kind, idx):
            if (kind, idx) in pend:
                return pend.pop((kind, idx))
            return R.get(kind, idx)

        def fm_group(b, wv_, c0, act, nk):
            ent = tm_cache.get(id(wv_))
            if ent is None:
                for kc in range(nk):
                    O.mm(pst[:, b, 0:Tt], lhsT=wv_[:, kc, c0:c0 + 128], rhs=act[:, kc, :], start=(kc == 0), stop=(kc == nk - 1))
            else:
                O.tr(pst[:, b, 0:Tt], ent[0][:, c0:c0 + 128], ident_f)

        if tl["kind"] == "p":
            p0 = tl["ti"] * T
            O.dma("sp", cosT[:, 0:Tt], rope_in[0, :, p0:p0 + Tt], key="cos")
            O.dma("sp", sinT[:, 0:Tt], rope_in[1, :, p0:p0 + Tt], key="sin")
        else:
            for si in range(nseg):
                O.dma("sp", cosT[:, si * L:(si + 1) * L], rope_in[0, :, SEQ:SEQ + L], key="cos", chain=(si == 0))
                O.dma("sp", sinT[:, si * L:(si + 1) * L], rope_in[1, :, SEQ:SEQ + L], key="sin", chain=(si == 0))

        if tix == 0:
            TM.reset()
            xn = P.sb("xn_%d" % tix, [128, TB, D], F32, Y.base)
            norm_phase(tl, xt, xn, hT, A1T, sh1)

        X.reset()
        Y.reset()
        qT = sbt(X, [128, QC, Tt], BF16, "qT")
        oT = sbt(X, [128, QC, Tt], BF16, "oT")
        sT = sbt(X, [128, CC, Tt], BF16, "sT")
        kT2 = sbt(X, [128, NKV, nseg, 128 + L], BF16, "kT2")
        gluT = sbt(Y, [128, CC, nseg, HK + L], BF16, "gluT")
        Vz = sbt(Y, [64, nseg, 2 + NCH, KVW], BF16, "Vz")
        dw = sbt(Y, [128, CC, Tt], F32, "dw")
        sqb = sbt(Y, [128, CC, Tt], BF16, "sqb")
        TAPG = 16
        dgb = [sbt(Y, [128, TAPG, 128], BF16, "dg0"), None]
        assert Y.cur <= Y.base + 42 * 1024, Y.cur - Y.base

        if tl["kind"] == "p":
            if tl["first"]:
                O.memset(gluHb[:, 0, :, :], 0.0, eng="pool")
                O.memset(ffnH[:, 0, :, :], 0.0, eng="pool")
            else:
                O.copy(kT2[:, :, 0, 0:128], kH[:, :, :], eng="pool")
                O.copy(Vz[:, 0, 0:2, :], vH[:, :, :], eng="pool")
        else:
            for si in range(nseg):
                TM.reset()
                st = sbt(TM, [32, DC], F32, "cst_cv")
                O.dma("pool", st[0:HK, :], cconv[si, :, :], key="h_cv")
                b = bank()
                for c in range(CC):
                    O.tr(pst[:, b, c * 32:c * 32 + HK], st[0:HK, c * 128:(c + 1) * 128], ident_f[0:HK, 0:HK])
                O.act(gluHb[:, si, :, :], pst[:, b, 0:CC * 32].rearrange("p (c k) -> p c k", k=32)[:, :, 0:HK], AF.Copy)
                stk = sbt(TM, [128, KVW], F32, "cst_k")
                O.dma("pool", stk[:, :], ck_in[si, :, :], key="h_k")
                stkd = sbt(TM, [128, NKV, 2, 64], F32, "cst_kd")
                O.copy(stkd[:, :, :, :], stk[:, :].rearrange("p (g o d) -> p g o d", o=1, d=64).to_broadcast([128, NKV, 2, 64]), eng="dve")
                for g in range(NKV):
                    b = bank()
                    O.tr(pst[:, b, 0:128], stkd[:, g, :, :].rearrange("p o d -> p (o d)"), ident_f)
                    O.act(kT2[:, g, si, 0:128], pst[:, b, 0:128], AF.Copy)
                stv = sbt(TM, [64, 2, KVW], F32, "cst_v")
                O.dma("pool", stv[:, :, :], cv_in[si, :, :].rearrange("(c p) w -> p c w", p=64), key="h_v")
                O.copy(Vz[:, si, 0:2, :], stv[:, :, :], eng="dve")
                npc = 8
                cpp = (FFC + npc - 1) // npc
                stfs = [sbt(TM, [2, cpp * 128], F32, "cst_f%d" % i) for i in range(2)]
                for pc in range(npc):
                    c0 = pc * cpp
                    c1 = min(FFC, c0 + cpp)
                    if c0 >= c1:
                        continue
                    stf = stfs[pc % 2]
                    O.dma("pool", stf[:, 0:(c1 - c0) * 128], cffn[si, :, c0 * 128:c1 * 128], key="h_f%d" % (pc % 2))
                    b = bank()
                    for c in range(c0, c1):
                        O.tr(pst[:, b, (c - c0) * 2:(c - c0) * 2 + 2], stf[:, (c - c0) * 128:(c - c0 + 1) * 128], ident_f[0:2, 0:2])
                    O.act(ffnH[:, si, c0:c1, :], pst[:, b, 0:(c1 - c0) * 2].rearrange("p (c k) -> p c k", k=2), AF.Copy)
                O.dma("pool", nk_s[si, 0:C.WIN - DEC, :], ck_in[si, DEC:C.WIN, :], key="d2d")
                O.dma("pool", nv_s[si, 0:C.WIN - DEC, :], cv_in[si, DEC:C.WIN, :], key="d2d")
        for si in range(nseg):
            O.copy(gluT[:, :, si, 0:HK], gluHb[:, si, :, :], eng="pool")

        TM.reset()
        Hbufs[1].reset()
        HT = Hbufs[1]
        tA = [sbt(TM, [128, Tt], F32, "tA0"), sbt(HT, [128, Tt], F32, "tA1")]
        dgb[1] = sbt(TM, [128, TAPG, 128], BF16, "dg1")
        cnt = [0]

        def seg_view(ap2d):
            return ap2d.rearrange("p (s l) -> p s l", l=L)

        bg = []

        def pump(n):
            for _ in range(n):
                if not bg:
                    return
                bg.pop(0)()

        for s in range(DC // 256):
            wv = R.get("A", s)
            for i in range(2):
                c = 2 * s + i
                ba, bb = bank(), bank()
                for kc in range(FC):
                    O.mm(pst[:, ba, 0:Tt], lhsT=wv[:, kc, i * 128:(i + 1) * 128], rhs=hT[:, kc, :], start=(kc == 0), stop=(kc == FC - 1))
                for kc in range(FC):
                    O.mm(pst[:, bb, 0:Tt], lhsT=wv[:, kc, 256 + i * 128:256 + (i + 1) * 128], rhs=hT[:, kc, :], start=(kc == 0), stop=(kc == FC - 1))
                sg = tA[cnt[0] % 2]
                cnt[0] += 1
                O.act(sg[:, :], pst[:, bb, 0:Tt], AF.Sigmoid, bias=vcol("b_in", (o_zb // 128) + c))
                O.stt(gluT[:, c, :, HK:HK + L], seg_view(pst[:, ba, 0:Tt]), vcol("b_in", c), seg_view(sg[:, :]), ALU.add, ALU.mult)
                if tl["last"]:
                    O.stt(gluH[:, 0:nseg, c, :], seg_view(pst[:, ba, 0:Tt])[:, :, L - HK:L], vcol("b_in", c),
                          seg_view(sg[:, :])[:, :, L - HK:L], ALU.add, ALU.mult)
            R.done()
        CONV_BANKS = [6, 7]

        def conv_gen():
            groups = [(c, k0) for c in range(CC) for k0 in range(0, CK, TAPG)]

            def build(gi):
                c, k0 = groups[gi]
                n = min(TAPG, CK - k0)
                j0 = vrows["conv_w"] + k0 * CC + c
                wtap = vecT[:, j0:j0 + (n - 1) * CC + 1:CC].unsqueeze(2).to_broadcast([128, n, 128])
                O.tt(dgb[gi % 2][:, 0:n, :], ident_b.unsqueeze(1).to_broadcast([128, n, 128]), wtap, ALU.mult)

            build(0)
            for gi, (c, k0) in enumerate(groups):
                if nseg == 1:
                    bks = [CONV_BANKS[c % 2]]
                else:
                    bks = CONV_BANKS[:nseg]
                n = min(TAPG, CK - k0)
                dg = dgb[gi % 2]
                for k in range(k0, k0 + n):
                    for si in range(nseg):
                        O.mm(pst[:, bks[si], 0:L], lhsT=dg[:, k - k0, :], rhs=gluT[:, c, si, k:k + L],
                             start=(k == 0), stop=(k == CK - 1))
                    if k == k0 and gi + 1 < len(groups):
                        build(gi + 1)
                    yield
                if k0 + n >= CK:
                    for si in range(nseg):
                        O.act(dw[:, c, si * L:(si + 1) * L], pst[:, bks[si], 0:L], AF.Identity, bias=vcol("conv_b", c))
                        O.act(sqb[:, c, si * L:(si + 1) * L], pst[:, bks[si], 0:L], AF.Square, bias=vcol("conv_b", c))
            for si in range(nseg):
                O.copy(gluHb[:, si, :, :], gluT[:, :, si, L:L + HK], eng="pool")

        last = tl["last"]
        if last:
            kst = sbt(TM, [128, KVW], F32, "kst")

        wv_kv = [None]
        kf = [sbt(TM, [128, Tt], F32, "kf%d" % i) for i in range(KC2)]
        kb = [sbt(HT, [128, Tt], BF16, "kb%d" % i) for i in range(KC2)]
        raws = [sbt(HT, [128, Tt], F32, "qraw%d" % i) for i in range(3)]
        rss = [sbt(HT, [128, Tt], F32, "qrs%d" % i) for i in range(2)]
        sqs = [sbt(HT, [128, Tt], BF16, "qsq%d" % i) for i in range(2)]
        nq = QC
        chunks = []
        for j in range(QC):
            chunks.append(("q", j, vcol("b_in", o_q // 128 + j), qkg[:, 0:1], qT[:, j, :], None))
        for j in range(KC2):
            chunks.append(("k", j, vcol("b_in", o_k // 128 + j), qkg[:, 1:2], kb[j][:, :], kf[j][:, :]))
        nchunk = len(chunks)
        st = {}
        wq = [None]

        def stage_a(n):
            kind, j, bcol, gcol, dst_bf, dst_f32 = chunks[n]
            if kind == "q":
                if j % 4 == 0:
                    wq[0] = R.get("Q", j // 4)
                wv, c0 = wq[0], (j % 4) * 128
            else:
                if j == 0:
                    wv_kv[0] = R.get("KV", 0)
                wv, c0 = wv_kv[0], j * 128
            b = bank()
            for kc in range(FC):
                O.mm(pst[:, b, 0:Tt], lhsT=wv[:, kc, c0:c0 + 128], rhs=hT[:, kc, :], start=(kc == 0), stop=(kc == FC - 1))
            if kind == "q" and j % 4 == 3:
                R.done()
            raw, sq = raws[n % 3], sqs[n % 2]
            O.act(raw[:, :], pst[:, b, 0:Tt], AF.Identity, bias=bcol)
            O.act(sq[:, :], pst[:, b, 0:Tt], AF.Square, bias=bcol)

        def stage_b(n):
            kind, j, bcol, gcol, dst_bf, dst_f32 = chunks[n]
            raw, sq, rs = raws[n % 3], sqs[n % 2], rss[n % 2]
            b2 = bank()
            O.mm(pst[:, b2, 0:Tt], lhsT=onesblk_b, rhs=sq[:, :])
            rsqrt_act(rs[:, :], pst[:, b2, 0:Tt], 1.0, 128)
            O.stt(raw[:, :], raw[:, :], gcol, rs[:, :], ALU.mult, ALU.mult)

        def stage_c(n):
            kind, j, bcol, gcol, dst_bf, dst_f32 = chunks[n]
            qn, t1 = raws[n % 3], rss[n % 2]
            b3 = bank()
            O.mm(pst[:, b3, 0:Tt], lhsT=rotT_f, rhs=qn[:, :])
            O.tt(t1[:, :], qn[:, :], cosT[:, 0:Tt], ALU.mult)
            O.tt(qn[:, :], pst[:, b3, 0:Tt], sinT[:, 0:Tt], ALU.mult)
            if dst_f32 is None:
                O.tt(dst_bf, t1[:, :], qn[:, :], ALU.add)
            else:
                O.tt(dst_f32, t1[:, :], qn[:, :], ALU.add)
                O.act(dst_bf, dst_f32, AF.Copy)

        for step in range(nchunk + 2):
            if step < nchunk:
                stage_a(step)
            if 0 <= step - 1 < nchunk:
                stage_b(step - 1)
            if 0 <= step - 2 < nchunk:
                stage_c(step - 2)
        wv = wv_kv[0]
        for g in range(NKV):
            b = bank()
            O.mm(pst[:, b, 0:Tt], lhsT=dup_b[g % 2], rhs=kb[g // 2][:, :])
            O.act(kT2[:, g, :, 128:128 + L], seg_view(pst[:, b, 0:Tt]), AF.Copy)
        if last:
            for si in range(nseg):
                nrows = min(C.WIN, L)
                b = bank()
                for j in range(KC2):
                    O.tr(pst[0:nrows, b, j * 128:(j + 1) * 128], kf[j][:, (si + 1) * L - nrows:(si + 1) * L], ident_f)
                O.act(kst[0:nrows, :], pst[0:nrows, b, 0:KVW], AF.Copy)
                if tl["kind"] == "p":
                    O.dma("pool", nk_p[tl["s"], :, :], kst[0:nrows, :], key="kst")
                else:
                    O.dma("pool", nk_s[si, C.WIN - DEC:C.WIN, :], kst[0:nrows, :], key="kst")
        if last:
            vst = sbt(TM, [64, 2, KVW], F32, "vst")
        for si in range(nseg):
            for ch in range(NCH):
                b = bank()
                c0 = si * L + ch * 64
                for kc in range(FC):
                    O.mm(pst[0:64, b, 0:KVW], lhsT=hT[:, kc, c0:c0 + 64], rhs=wv[:, kc, KVW:2 * KVW], start=(kc == 0), stop=(kc == FC - 1))
                O.tt(Vz[:, si, 2 + ch, :], pst[0:64, b, 0:KVW], bvb[:, :], ALU.add)
                if last and ch >= NCH - 2:
                    slot_ = ch - (NCH - 2) if NCH >= 2 else 0
                    O.tt(vst[:, slot_, :], pst[0:64, b, 0:KVW], bvb[:, :], ALU.add)
                pump(3)
            if last:
                if tl["kind"] == "p":
                    O.dma("pool", nv_p[tl["s"], :, :].rearrange("(c p) w -> p c w", p=64), vst[:, :, :], key="vst")
                else:
                    O.dma("pool", nv_s[si, C.WIN - DEC:C.WIN, :], vst[:, 0, :], key="vst")
        R.done()

        pT_all = sbt(TM, [64, 4, 3, 2, 64], BF16, "pT")
        pT = [[pT_all[:, 2 * i + h, :, :, :] for h in range(2)] for i in range(2)]
        rden = [sbt(TM, [128, 2, 64], F32, "rden%d" % i) for i in range(2)]
        its = []
        for si in range(nseg):
            for ci in range(NCH):
                kk_list = [ci, ci + 1, ci + 2]
                if tl["kind"] == "p" and tl["first"]:
                    kk_list = [k_ for k_ in kk_list if k_ >= 2]
                for g in range(NKV):
                    its.append((si, ci, g, kk_list))

        ATT_BANKS = [[0, 1, 2], [3, 4, 5]]
        pTv = [pT_all[:, 2 * i:2 * i + 2, :, :, :] for i in range(2)]

        def att_scores(n):
            si, ci, g, kk_list = its[n]
            par = n % 2
            nk = len(kk_list)
            qcols = slice(si * L + ci * 64, si * L + (ci + 1) * 64)
            for half in range(2):
                hp = slice(half * 64, (half + 1) * 64)
                bS = ATT_BANKS[par][half]
                psS = pst[0:64, bS, 0:384].rearrange("p (k j q) -> p k j q", j=2, q=64)
                for ks, kk in enumerate(kk_list):
                    O.mm(psS[:, ks, :, :], lhsT=kT2[hp, g, si, kk * 64:(kk + 1) * 64],
                         rhs=qT[hp, 2 * g:2 * g + 2, qcols])
                O.act(pT[par][half][:, 0:nk, :, :], psS[:, 0:nk, :, :], AF.Exp, scale=HD ** -0.5)

        def att_pv(n):
            si, ci, g, kk_list = its[n]
            par = n % 2
            nk = len(kk_list)
            qcols = slice(si * L + ci * 64, si * L + (ci + 1) * 64)
            bO = ATT_BANKS[par][2]
            psO = pst[:, bO, 0:128].rearrange("p (j q) -> p j q", q=64)
            psD = pst[:, bO, 128:384].rearrange("p (h j q) -> p h j q", h=2, q=64)
            for half in range(2):
                hp = slice(half * 64, (half + 1) * 64)
                for ks, kk in enumerate(kk_list):
                    O.mm(psO[hp, :, :], lhsT=Vz[:, si, kk, g * 64:(g + 1) * 64], rhs=pT[par][half][:, ks, :, :],
                         start=(ks == 0), stop=(ks == nk - 1))
            for ks, kk in enumerate(kk_list):
                O.mm(psD, lhsT=ones_b[0:64, :], rhs=pTv[par][:, :, ks, :, :], start=(ks == 0), stop=(ks == nk - 1))
            for half in range(2):
                hp = slice(half * 64, (half + 1) * 64)
                O.tt(rden[par][hp, :, :], psD[hp, half, :, :], sinkexp[hp, g, :, :], ALU.add)
            O.recip(rden[par][:, :, :], rden[par][:, :, :])
            O.tt(oT[:, 2 * g:2 * g + 2, qcols], psO, rden[par][:, :, :], ALU.mult)

        cg = conv_gen()
        n_conv_steps = CC * CK
        per_it = (n_conv_steps + len(its) - 1) // len(its)
        att_scores(0)
        for n in range(len(its)):
            if n + 1 < len(its):
                att_scores(n + 1)
            for _ in range(per_it):
                next(cg, None)
            att_pv(n)
            if tix == 0 and n % 2 == 1:
                R.consume_mod(1, b=ATT_BANKS[(n + 1) % 2][2])
        for _ in cg:
            pass
        if tix == 0:
            R.consume_mod(nmslab)
        if tl["kind"] == "p" and not tl["last"]:
            O.copy(kH[:, :, :], kT2[:, :, 0, L:L + 128], eng="pool")
            O.copy(vH[:, :, :], Vz[:, 0, NCH:NCH + 2, :], eng="pool")
        pump(10 ** 6)

        if last:
            TM.reset()
            for si in range(nseg):
                cvst = sbt(TM, [32, DC], F32, "cvst")
                for c4 in range((CC + 3) // 4):
                    b = bank()
                    n4 = min(4, CC - c4 * 4)
                    for i in range(n4):
                        c = c4 * 4 + i
                        O.tr(pst[0:HK, b, i * 128:(i + 1) * 128], gluH[:, si, c, :], ident_f)
                    O.act(cvst[0:HK, c4 * 512:c4 * 512 + n4 * 128], pst[0:HK, b, 0:n4 * 128], AF.Copy)
                if tl["kind"] == "p":
                    O.dma("pool", ncv_p[tl["s"], :, :], cvst[0:HK, :], key="cvst%d" % si)
                else:
                    O.dma("pool", ncv_s[si, :, :], cvst[0:HK, :], key="cvst%d" % si)

        TM.reset()
        mean_sb = sbt(TM, [128, Tt], F32, "mean")
        var_sb = sbt(TM, [128, Tt], F32, "var")
        rstd_l = sbt(TM, [128, Tt], F32, "rstdl")
        nmr = sbt(TM, [128, Tt], F32, "nmr")
        tt_ = [sbt(TM, [128, Tt], F32, "lnt%d" % i) for i in range(2)]
        bm, bx = bank(), bank()
        for c in range(CC):
            O.mm(pst[:, bm, 0:Tt], lhsT=oneln_f[:, :], rhs=dw[:, c, :], start=(c == 0), stop=(c == CC - 1))
        for c in range(CC):
            O.mm(pst[:, bx, 0:Tt], lhsT=oneln_b[:, :], rhs=sqb[:, c, :], start=(c == 0), stop=(c == CC - 1))
        O.act(mean_sb[:, :], pst[:, bm, 0:Tt], AF.Copy)
        O.act(var_sb[:, :], pst[:, bm, 0:Tt], AF.Square)
        O.tt(var_sb[:, :], pst[:, bx, 0:Tt], var_sb[:, :], ALU.subtract)
        rsqrt_act(rstd_l[:, :], var_sb[:, :], 1.0, 128)
        O.stt(nmr[:, :], mean_sb[:, :], -1.0, rstd_l[:, :], ALU.mult, ALU.mult)
        for c in range(CC):
            t_ = tt_[c % 2]
            O.tt(t_[:, :], dw[:, c, :], rstd_l[:, :], ALU.mult)
            O.tt(t_[:, :], t_[:, :], nmr[:, :], ALU.add)
            O.act(sT[:, c, :], t_[:, :], AF.Silu, bias=vcol("ln_b", c), scale=vcol("ln_g", c))

        xres = P.sb("xres_%d" % tix, [128, TB, D], F32, Y.base)
        O.dma("pool", xres[:, :, :], x_src(tl), key="xres")

        TM.reset()
        sgt = sbt(TM, [128, 4, Tt], F32, "sgt")
        m1t = sbt(TM, [128, 4, Tt], F32, "m1t")
        m2t = sbt(X, [128, Tt], F32, "m2t")
        for g in range(D // 512):
            wv = getp("GC", g)
            pre("CO", g, sT, CC)
            for i in range(4):
                j = 4 * g + i
                b = bank()
                fm_group(b, wv, i * 128, hT, FC)
                O.act(sgt[:, i, :], pst[:, b, 0:Tt], AF.Sigmoid, bias=vcol("b_in", o_gc // 128 + j))
            R.done()
            wv = getp("CO", g)
            pre("GA", g, hT, FC)
            for i in range(4):
                b = bank()
                fm_group(b, wv, i * 128, sT, CC)
                O.tt(m1t[:, i, :], pst[:, b, 0:Tt], sgt[:, i, :], ALU.mult)
            R.done()
            wv = getp("GA", g)
            pre("AO", g, oT, QC)
            for i in range(4):
                j = 4 * g + i
                b = bank()
                fm_group(b, wv, i * 128, hT, FC)
                O.act(sgt[:, i, :], pst[:, b, 0:Tt], AF.Sigmoid, bias=vcol("b_in", o_ga // 128 + j))
            R.done()
            wv = getp("AO", g)
            if g + 1 < D // 512:
                pre("GC", g + 1, hT, FC)
            for i in range(4):
                j = 4 * g + i
                b = bank()
                fm_group(b, wv, i * 128, oT, QC)
                O.tt(m2t[:, :], pst[:, b, 0:Tt], sgt[:, i, :], ALU.mult)
                O.tt(mT[:, j, :], m1t[:, i, :], m2t[:, :], ALU.add)
            R.done()
            if g == 0:
                build_gbc(tl, g1T)

        TM.reset()
        ep = [sbt(TM, [128, 512], F32, "ep%d" % i) for i in range(2)]
        sjunk = sbt(TM, [128, 512], BF16, "sjunk")
        ssqp = sbt(TM, [128, TB, D // 512], F32, "ssqp")
        x1 = P.sb("x1_%d" % tix, [128, TB, D], F32, X.base)
        e_i = 0
        for g in range(D // 512):
            wv = R.get("WO", g)
            bks = [bank() for _ in range(TB)]
            for tb in range(TB):
                for kc in range(FC):
                    O.mm(pst[:, bks[tb], :], lhsT=mT[:, kc, tb * 128:(tb + 1) * 128], rhs=wv[:, kc, :], start=(kc == 0), stop=(kc == FC - 1))
                e = ep[e_i % 2]
                e_i += 1
                O.tt(e[:, :], pst[:, bks[tb], :], gbc[:, g * 512:(g + 1) * 512], ALU.mult)
                O.tt(x1[:, tb, g * 512:(g + 1) * 512], e[:, :], xres[:, tb, g * 512:(g + 1) * 512], ALU.add)
                O.act(sjunk[:, :], x1[:, tb, g * 512:(g + 1) * 512], AF.Square, accum_out=ssqp[:, tb, g:g + 1])
            R.done()
        G_ = D // 512
        if G_ == 1:
            O.copy(ssq[:, 0:TB], ssqp[:, :, 0], eng="dve")
        else:
            O.tt(ssq[:, 0:TB], ssqp[:, :, 0], ssqp[:, :, 1], ALU.add)
            for g_ in range(2, G_):
                O.tt(ssq[:, 0:TB], ssq[:, 0:TB], ssqp[:, :, g_], ALU.add)

        TM.reset()
        if tix == 0:
            assert all(k_[0] != "MOD" for k_ in slabs[R.cons:]), "mod slabs must be consumed before norm2"
            O.ts(A2T[:, :, :], sc2, 1.0, None, ALU.add)
            O.tt(A2T[:, :, :], A2T[:, :, :], n2g, ALU.mult)
        xn2 = P.sb("xn2_%d" % tix, [128, TB, D], F32, Y.base)
        norm_phase(tl, x1, xn2, h2T, A2T, sh2, ssq_done=True)
        build_gbc(tl, g2T)

        TM.reset()
        aT = P.sb("aT_%d" % tix, [128, FFC, Tt], BF16, Y.base)
        assert FFC * Tt * 2 <= 44 * 1024
        usb = [sbt(TM, [128, nseg, FK - 1 + L], F32, "usb%d" % i) for i in range(2)]
        acc_ = [sbt(TM, [128, Tt], F32, "facc%d" % i) for i in range(2)]
        sg_ = [sbt(TM, [128, Tt], F32, "fsg%d" % i) for i in range(2)]
        for s in range(DFF // 256):
            if s == 0:
                pre("UP", 0, h2T, FC)
            wv = getp("UP", s)
            if s + 1 < DFF // 256:
                pre("UP", s + 1, h2T, FC)
            for i in range(2):
                c = 2 * s + i
                pr = c % 2
                bg_, bv_ = bank(), bank()
                fm_group(bg_, wv, i * 128, h2T, FC)
                fm_group(bv_, wv, 256 + i * 128, h2T, FC)
                u = usb[pr]
                a = acc_[pr]
                av = seg_view(a[:, :])
                O.copy(u[:, :, 0:FK - 1], ffnH[:, 0:nseg, c, :], eng="dve")
                O.act(u[:, :, FK - 1:FK - 1 + L], seg_view(pst[:, bg_, 0:Tt]), AF.Copy)
                O.act(a[:, :], pst[:, bg_, 0:Tt], AF.Identity, bias=vcol("ffn_conv_b", c), scale=vcol("ffn_conv_w", (FK - 1) * FFC + c))
                for k in range(FK - 1):
                    O.stt(av, u[:, :, k:k + L], vcol("ffn_conv_w", k * FFC + c), av, ALU.mult, ALU.add)
                O.copy(ffnH[:, 0:nseg, c, :], u[:, :, L:L + FK - 1], eng="dve")
                O.act(sg_[pr][:, :], a[:, :], AF.Silu)
                O.tt(aT[:, c, :], sg_[pr][:, :], pst[:, bv_, 0:Tt], ALU.mult)
            R.done()
        if last:
            fst = [sbt(TM, [FK - 1, 512], F32, "fst%d" % i) for i in range(2)]
            fi = 0
            for si in range(nseg):
                for c4 in range((FFC + 3) // 4):
                    n4 = min(4, FFC - c4 * 4)
                    b = bank()
                    for i in range(n4):
                        O.tr(pst[0:FK - 1, b, i * 128:(i + 1) * 128], ffnH[:, si, c4 * 4 + i, :], ident_f)
                    f_ = fst[fi % 2]
                    fi += 1
                    O.act(f_[:, 0:n4 * 128], pst[0:FK - 1, b, 0:n4 * 128], AF.Copy)
                    dst = nf_p[tl["s"]] if tl["kind"] == "p" else nf_s[si]
                    O.dma("pool", dst[:, c4 * 512:c4 * 512 + n4 * 128], f_[:, 0:n4 * 128], key="fst%d" % (fi % 2))

        TM.reset()
        ep = [sbt(TM, [128, 512], F32, "epd%d" % i) for i in range(2)]
        sjunk1 = sbt(TM, [128, 512], BF16, "sjunk1")
        has_next = tix + 1 < len(tiles)
        G_ = D // 512
        filler = None
        side = {}
        if has_next:
            ntl = tiles[tix + 1]
            nTB = ntl["T"] // 128
            nxt_xt = x_tensor(ntl, X.base, "xt_%d" % (tix + 1))
            hT_next = P.sb("hTn_%d" % tix, [128, FC, ntl["T"]], BF16, Hbufs[0].base)
            xsrc = {}
            side = {}
            side[G_ - 1] = sbt(TM, [128, nTB, 512], F32, "xtail")
            if G_ >= 2:
                side[G_ - 2] = P.sb("xtail2_%d" % tix, [128, nTB, 512], F32, Hbufs[1].base)
            for g_ in range(G_):
                xsrc[g_] = side[g_][:, :, :] if g_ in side else nxt_xt[:, :, g_ * 512:(g_ + 1) * 512]
            for g_ in sorted(side):
                O.dma("pool", xsrc[g_], x_src(ntl)[:, :, g_ * 512:(g_ + 1) * 512], key="xl%d" % g_)
                for tb in range(nTB):
                    j_ = 8 + tb * G_ + g_
                    O.act(sjunk1[:, :], xsrc[g_][:, tb, :], AF.Square, accum_out=ssq[:, j_:j_ + 1])
            filler = norm1_prefetch_gen(ntl, xsrc, hT_next)
        for g in range(G_):
            gc_ = slice(g * 512, (g + 1) * 512)
            lastg = (g == G_ - 1)
            tot_mm = FFC * TB
            fill_start = tot_mm // 5
            fill_every = max(1, (tot_mm - fill_start - 8) // (FC + 1))
            if lastg:
                bks = list(range(TB))
            else:
                bks = [bank() for _ in range(TB)]
            k0 = 0
            nmm = 0
            while k0 < FFC:
                nk = min(16, FFC - k0)
                wv = R.get("DN", (g, k0))
                for tb in range(TB):
                    for kc in range(nk):
                        O.mm(pst[:, bks[tb], :], lhsT=aT[:, k0 + kc, tb * 128:(tb + 1) * 128], rhs=wv[:, kc, :],
                             start=(k0 + kc == 0), stop=(k0 + kc == FFC - 1))
                        nmm += 1
                        if lastg and filler is not None:
                            if nmm == 1 or (nmm >= fill_start and (nmm - fill_start) % fill_every == 0):
                                next(filler, None)
                R.done()
                k0 += nk
            if lastg and filler is not None:
                for _ in filler:
                    pass
            for tb in range(TB):
                e = ep[e_i % 2]
                e_i += 1
                O.tt(e[:, :], pst[:, bks[tb], :], gbc[:, gc_], ALU.mult)
                O.tt(x1[:, tb, gc_], e[:, :], x1[:, tb, gc_], ALU.add)
            O.dma("pool", y_dst(tl)[:, :, gc_], x1[:, :, gc_], key="ys%d" % g)
            if has_next and g not in side:
                O.dma("pool", xsrc[g], x_src(ntl)[:, :, gc_], key="xl%d" % g)
                for tb in range(nTB):
                    j_ = 8 + tb * G_ + g
                    O.act(sjunk1[:, :], xsrc[g][:, tb, :], AF.Square, accum_out=ssq[:, j_:j_ + 1])

    P.finalize(final_wait_eng="pool")
    return nc, P


def host_consts(C):
    cst = np.zeros((128, 5 * 128), np.float32)
    cst[:, 0:128] = np.eye(128, dtype=np.float32)
    rot = np.zeros((128, 128), np.float32)
    for m in range(128):
        d = m % 64
        base = m - d
        if d < 32:
            rot[base + d + 32, m] = -1.0
        else:
            rot[base + d - 32, m] = 1.0
    cst[:, 128:256] = rot
    blk = np.zeros((128, 128), np.float32)
    blk[0:64, 0:64] = 1.0 / 64
    blk[64:128, 64:128] = 1.0 / 64
    cst[:, 256:384] = blk
    for h in range(2):
        dp = np.zeros((128, 128), np.float32)
        for m in range(128):
            dp[h * 64 + (m % 64), m] = 1.0
        cst[:, 384 + 128 * h:512 + 128 * h] = dp
    half = C.HD // 2
    inv_freq = (1.0 / (np.float32(C.THETA) ** (np.arange(half, dtype=np.float32) / np.float32(half)))).astype(np.float32)
    pos = np.concatenate([np.arange(C.SEQ, dtype=np.float32), C.PAST + np.arange(C.DEC, dtype=np.float32)])
    ang = (pos[None, :] * inv_freq[:, None]).astype(np.float32)
    idx = (np.arange(128) % 64) % 32
    rope = np.stack([np.cos(ang)[idx], np.sin(ang)[idx]]).astype(np.float32)
    return cst, rope


_CACHE = {}


def kernel(**inputs):
    C = Cfg
    f = lambda k: np.ascontiguousarray(np.asarray(inputs[k], dtype=np.float32))
    if "nc" not in _CACHE:
        _CACHE["nc"] = build_program(C)[0]
    nc = _CACHE["nc"]
    cst, rope = host_consts(C)
    x_prompt, x_sample = f("x_prompt"), f("x_sample")
    c_prompt, c_sample = f("c_prompt"), f("c_sample")
    cache_conv, cache_k, cache_v, cache_ffn = f("cache_conv")[0], f("cache_k")[0], f("cache_v")[0], f("cache_ffn_conv")[0]
    shared = {
        "mod_w": f("mod_w")[0], "mod_b": f("mod_b")[0], "norm1_g": f("norm1_g")[0], "w_in": f("w_in")[0],
        "b_in": f("b_in")[0], "conv_w": f("conv_w")[0], "conv_b": f("conv_b")[0], "ln_g": f("ln_g")[0],
        "ln_b": f("ln_b")[0], "conv_out_w": f("conv_out_w")[0], "q_norm_g": f("q_norm_g")[0],
        "k_norm_g": f("k_norm_g")[0], "sinks": f("sinks")[0], "attn_o_w": f("attn_o_w")[0], "w_out": f("w_out")[0],
        "norm2_g": f("norm2_g")[0], "ffn_up_w": f("ffn_up_w")[0], "ffn_conv_w": f("ffn_conv_w")[0],
        "ffn_conv_b": f("ffn_conv_b")[0], "ffn_down_w": f("ffn_down_w")[0], "cst": cst, "rope": rope,
    }
    NS, NM = C.NSEQ, C.NSMP
    in_maps = []
    for c in range(N_CORES):
        m = dict(shared)
        m["xp"] = x_prompt[c * NS:(c + 1) * NS]
        m["xs"] = x_sample[c * NM:(c + 1) * NM].reshape(NM * C.DEC, C.D)
        m["call"] = np.concatenate([c_prompt[c * NS:(c + 1) * NS], c_sample[c * NM:(c + 1) * NM]], axis=0)
        m["cconv"] = cache_conv[c * NM:(c + 1) * NM]
        m["ck"] = cache_k[c * NM:(c + 1) * NM].reshape(NM, C.WIN, C.NKV * C.HD)
        m["cv"] = cache_v[c * NM:(c + 1) * NM].reshape(NM, C.WIN, C.NKV * C.HD)
        m["cffn"] = cache_ffn[c * NM:(c + 1) * NM]
        in_maps.append({k: np.ascontiguousarray(v) for k, v in m.items()})
    res = run_bass_kernel_spmd(nc, in_maps, core_ids=list(range(N_CORES)))
    rs = res.results
    cat = lambda k: np.concatenate([np.asarray(r[k]) for r in rs], axis=0)
    B, BS = N_CORES * NS, N_CORES * NM
    y_p = cat("yp").reshape(B, C.SEQ, C.D)
    y_s = cat("ys").reshape(BS, C.DEC, C.D)
    return (
        y_p.astype(np.float32), y_s.astype(np.float32),
        cat("ncv_p").reshape(1, B, C.CK - 1, C.DC), cat("ncv_s").reshape(1, BS, C.CK - 1, C.DC),
        cat("nk_p").reshape(1, B, C.WIN, C.NKV, C.HD), cat("nk_s").reshape(1, BS, C.WIN, C.NKV, C.HD),
        cat("nv_p").reshape(1, B, C.WIN, C.NKV, C.HD), cat("nv_s").reshape(1, BS, C.WIN, C.NKV, C.HD),
        cat("nf_p").reshape(1, B, C.FK - 1, C.DFF), cat("nf_s").reshape(1, BS, C.FK - 1, C.DFF),
    )
```

```python
import numpy as np
from contextlib import ExitStack

import concourse.bass as bass
import concourse.mybir as mybir
from concourse.bass_utils import run_bass_kernel_spmd

F32 = mybir.dt.float32
BF16 = mybir.dt.bfloat16
AF = mybir.ActivationFunctionType
ALU = mybir.AluOpType

N_CORES = 8
EPS = 1e-6


class Cfg:
    D = 2048
    DC = 1024
    CK = 31
    NH = 16
    NKV = 4
    HD = 64
    DFF = 5632
    FK = 3
    SEQ = 2048
    DEC = 64
    PAST = 1024
    WIN = 128
    NSEQ = 2
    NSMP = 2
    T = 512
    THETA = 10000.0


BLK = 512
PSBANK = 2048


class Rec:
    __slots__ = ("eng", "emit", "stream", "ordinal", "deps", "waits", "inc", "incval", "vc", "isdma", "newgrp")


class Prog:
    ENG_ATTR = {"pe": "tensor", "act": "scalar", "dve": "vector", "pool": "gpsimd", "sp": "sync"}

    def __init__(self, nc):
        self.nc = nc
        self.recs = []
        self.count = {}
        self.lastw = {}
        self.readers = {}
        self.tinfo = {}
        self.tracked_dram = set()

    def sb(self, name, shape, dtype, offset):
        t = self.nc.alloc_sbuf_tensor_at(name, list(shape), dtype, offset=offset)
        es = 4 if dtype == F32 else 2
        self.tinfo[t.name] = ("sb", offset, es)
        return t

    def region(self, ap):
        name = ap.tensor.name
        info = self.tinfo.get(name)
        if info is None:
            if name in self.tracked_dram:
                return [("d:" + name, 0)]
            return []
        space, base, es = info
        pat = ap.ap
        pstep = pat[0][0]
        off = ap.offset % pstep if pstep > 0 else ap.offset
        gran = PSBANK if space == "ps" else BLK
        dims = sorted([(abs(st) * es, n) for st, n in pat[1:] if n > 1 and st != 0], reverse=True)
        out = set()

        def rec(lo, ds):
            ext = es
            for st, n in ds:
                ext += st * (n - 1)
            if ds and ds[0][0] >= gran and ds[0][1] <= 512:
                inner = es
                for st, n in ds[1:]:
                    inner += st * (n - 1)
                if ds[0][0] >= inner:
                    for i in range(ds[0][1]):
                        rec(lo + i * ds[0][0], ds[1:])
                    return
            for b_ in range(lo // gran, (lo + ext - 1) // gran + 1):
                out.add(b_)
        rec(base + off * es, dims)
        return [(space, b_) for b_ in sorted(out)]

    def op(self, eng, emit, reads=(), writes=(), key=None, chain=True):
        r = Rec()
        r.eng = eng
        r.emit = emit
        r.isdma = key is not None
        r.stream = ("dma:" + key) if key is not None else eng
        r.ordinal = self.count.get(r.stream, 0)
        self.count[r.stream] = r.ordinal + 1
        r.newgrp = bool(chain) or r.ordinal == 0
        deps = {}
        if r.isdma and r.newgrp and r.ordinal > 0:
            deps[r.stream] = r.ordinal - 1

        def add(s, o, e, raw):
            if s == eng and not r.isdma:
                if eng == "pe" or not raw:
                    return
            if r.isdma and s == r.stream:
                return
            if s not in deps or deps[s] < o:
                deps[s] = o

        rblocks = []
        for ap in reads:
            rblocks += self.region(ap)
        wblocks = []
        for ap in writes:
            wblocks += self.region(ap)
        for b in rblocks:
            w = self.lastw.get(b)
            if w is not None:
                add(w[0], w[1], w[2], True)
        for b in wblocks:
            w = self.lastw.get(b)
            if w is not None:
                add(w[0], w[1], w[2], False)
            rd = self.readers.get(b)
            if rd:
                for s, (o, e) in rd.items():
                    add(s, o, e, False)
        me = (r.stream, r.ordinal, eng)
        for b in rblocks:
            self.readers.setdefault(b, {})[r.stream] = (r.ordinal, eng)
        for b in wblocks:
            self.lastw[b] = me
            self.readers[b] = {}
        r.deps = deps
        self.recs.append(r)
        return r

    def finalize(self, final_wait_eng="pool"):
        nc = self.nc
        known = {e: {} for e in self.ENG_ATTR}
        needed = {}
        bystream = {}
        for r in self.recs:
            bystream.setdefault(r.stream, []).append(r)
        grp_end = {}
        for s, lst in bystream.items():
            if s.startswith("dma:"):
                ge = [0] * len(lst)
                end = len(lst) - 1
                for i in range(len(lst) - 1, -1, -1):
                    ge[i] = end
                    if lst[i].newgrp:
                        end = i - 1
                grp_end[s] = ge
        for r in self.recs:
            kn = known[r.eng]
            waits = []
            for s, o in r.deps.items():
                if s in grp_end:
                    o = grp_end[s][o]
                    assert not (s == r.stream and o >= r.ordinal)
                if kn.get(s, -1) >= o:
                    continue
                waits.append((s, o))
            for s, o in waits:
                needed.setdefault(s, set()).add(o)
                dep = bystream[s][o]
                if kn.get(s, -1) < o:
                    kn[s] = o
                for s2, o2 in dep.vc.items():
                    if kn.get(s2, -1) < o2:
                        kn[s2] = o2
            r.waits = waits
            r.vc = dict(kn)
        final = []
        for s, lst in bystream.items():
            if s.startswith("dma:"):
                needed[s] = set(range(len(lst)))
                final.append((s, len(lst) - 1))
        cnt = {}
        for s, lst in bystream.items():
            need = needed.get(s, set())
            c = 0
            tab = {}
            for r in lst:
                if r.ordinal in need:
                    c += 16 if r.isdma else 1
                    r.inc = True
                    tab[r.ordinal] = c
                else:
                    r.inc = False
            cnt[s] = tab
        self.nsem = len(bystream)
        with ExitStack() as es:
            sems = {}
            for i, s in enumerate(sorted(bystream)):
                sems[s] = es.enter_context(nc.semaphore("s%d" % i))
            block = es.enter_context(nc.Block())
            per_eng = {e: [] for e in self.ENG_ATTR}
            for r in self.recs:
                per_eng[r.eng].append(r)
            for e, attr in self.ENG_ATTR.items():
                lst = per_eng[e]
                fin = final if e == final_wait_eng else []

                def body(h, lst=lst, fin=fin):
                    for r in lst:
                        for s, o in r.waits:
                            h.wait_ge(sems[s], cnt[s][o])
                        ins = r.emit(h)
                        if r.inc:
                            ins.then_inc(sems[r.stream], 16 if r.isdma else 1)
                    for s, o in fin:
                        h.wait_ge(sems[s], cnt[s][o])
                if lst or fin:
                    getattr(block, attr)(body)


class Ops:
    def __init__(self, P):
        self.P = P

    def mm(self, out, lhsT, rhs, start=True, stop=True):
        return self.P.op("pe", lambda h: h.matmul(out, lhsT=lhsT, rhs=rhs, start=start, stop=stop),
                         reads=[lhsT, rhs], writes=[out])

    def tr(self, out, in_, ident):
        return self.P.op("pe", lambda h: h.transpose(out, in_, ident), reads=[in_, ident], writes=[out])

    def act(self, out, in_, func, bias=0.0, scale=1.0, accum_out=None):
        rd = [in_]
        if not isinstance(bias, (int, float)):
            rd.append(bias)
        if not isinstance(scale, (int, float)):
            rd.append(scale)
        wr = [out]
        if accum_out is not None:
            wr.append(accum_out)
        kw = {}
        if accum_out is not None:
            kw["accum_out"] = accum_out
        return self.P.op("act", lambda h: h.activation(out, in_, func, bias=bias, scale=scale, **kw),
                         reads=rd, writes=wr)

    def tt(self, out, in0, in1, op, eng="dve"):
        return self.P.op(eng, lambda h: h.tensor_tensor(out, in0, in1, op), reads=[in0, in1], writes=[out])

    def ts(self, out, in0, s1, s2, op0, op1=None, eng="dve"):
        rd = [in0]
        if not isinstance(s1, (int, float)):
            rd.append(s1)
        if s2 is not None and not isinstance(s2, (int, float)):
            rd.append(s2)
        if op1 is None:
            return self.P.op(eng, lambda h: h.tensor_scalar(out, in0, s1, None, op0), reads=rd, writes=[out])
        return self.P.op(eng, lambda h: h.tensor_scalar(out, in0, s1, s2, op0, op1), reads=rd, writes=[out])

    def stt(self, out, in0, scalar, in1, op0, op1):
        rd = [in0, in1]
        if not isinstance(scalar, (int, float)):
            rd.append(scalar)
        return self.P.op("dve", lambda h: h.scalar_tensor_tensor(out, in0, scalar, in1, op0, op1),
                         reads=rd, writes=[out])

    def copy(self, out, in_, eng="dve"):
        if eng == "act":
            return self.act(out, in_, AF.Copy)
        return self.P.op(eng, lambda h: h.tensor_copy(out, in_), reads=[in_], writes=[out])

    def memset(self, out, val, eng="pool"):
        return self.P.op(eng, lambda h: h.memset(out, val), writes=[out])

    def recip(self, out, in_):
        return self.P.op("dve", lambda h: h.reciprocal(out, in_), reads=[in_], writes=[out])

    def dma(self, eng, out, in_, key, chain=True):
        return self.P.op(eng, lambda h: h.dma_start(out=out, in_=in_), reads=[in_], writes=[out], key=key, chain=chain)


def ceil_to(x, a):
    return (x + a - 1) // a * a


class Arena:
    def __init__(self, base, size):
        self.base, self.size, self.cur = base, size, base

    def reset(self):
        self.cur = self.base

    def take(self, nbytes, align=BLK):
        off = ceil_to(self.cur, align)
        self.cur = off + nbytes
        assert self.cur <= self.base + self.size, ("arena overflow", self.cur - self.base, self.size)
        return off


def build_program(C):
    nc = bass.Bass("TRN2", target_bir_lowering=False)
    P = Prog(nc)
    O = Ops(P)

    D, DC, CK, NH, NKV, HD, DFF, FK = C.D, C.DC, C.CK, C.NH, C.NKV, C.HD, C.DFF, C.FK
    SEQ, DEC, NSEQ, NSMP, T = C.SEQ, C.DEC, C.NSEQ, C.NSMP, C.T
    assert HD == 64 and NH // NKV == 4 and DEC == 64 and NSMP == 2
    FC = D // 128
    CC = DC // 128
    AW = NH * HD
    KVW = NKV * HD
    QC = AW // 128
    KC2 = KVW // 128
    FFC = DFF // 128
    NIN = 2 * DC + AW + 2 * KVW + 2 * D
    NSQ = NSEQ + NSMP
    HK = CK - 1
    TSM = DEC * NSMP
    NT = SEQ // T
    o_zb, o_q, o_k, o_v, o_gc, o_ga = DC, 2 * DC, 2 * DC + AW, 2 * DC + AW + KVW, 2 * DC + AW + 2 * KVW, 2 * DC + AW + 2 * KVW + D
    assert DC % 256 == 0 and AW % 512 == 0 and D % 512 == 0 and DFF % 256 == 0 and KVW * 2 <= 512

    def din(name, shape, dt=F32):
        return nc.dram_tensor(name, list(shape), dt, kind="ExternalInput").ap()

    def dout(name, shape):
        return nc.dram_tensor(name, list(shape), F32, kind="ExternalOutput").ap()

    def dscr(name, shape, dt=BF16):
        t = nc.dram_tensor(name, list(shape), dt, kind="Internal")
        P.tracked_dram.add(t.name)
        return t.ap()

    xp = din("xp", [NSEQ, SEQ, D])
    xs = din("xs", [NSMP * DEC, D])
    call = din("call", [NSQ, D])
    cconv = din("cconv", [NSMP, HK, DC])
    ck_in = din("ck", [NSMP, C.WIN, KVW])
    cv_in = din("cv", [NSMP, C.WIN, KVW])
    cffn = din("cffn", [NSMP, FK - 1, DFF])
    mod_w = din("mod_w", [D, 6 * D])
    mod_b = din("mod_b", [6 * D])
    norm1_g = din("norm1_g", [D])
    w_in = din("w_in", [D, NIN])
    b_in = din("b_in", [NIN])
    conv_w = din("conv_w", [CK, DC])
    conv_b = din("conv_b", [DC])
    ln_g = din("ln_g", [DC])
    ln_b = din("ln_b", [DC])
    conv_out_w = din("conv_out_w", [DC, D])
    q_norm_g = din("q_norm_g", [HD])
    k_norm_g = din("k_norm_g", [HD])
    sinks = din("sinks", [NH])
    attn_o_w = din("attn_o_w", [AW, D])
    w_out = din("w_out", [D, D])
    norm2_g = din("norm2_g", [D])
    ffn_up_w = din("ffn_up_w", [D, 2 * DFF])
    ffn_conv_w = din("ffn_conv_w", [FK, DFF])
    ffn_conv_b = din("ffn_conv_b", [DFF])
    ffn_down_w = din("ffn_down_w", [DFF, D])
    NCST = 5 * 128
    cst_in = din("cst", [128, NCST])
    rope_in = din("rope", [2, 128, SEQ + DEC])

    yp = dout("yp", [NSEQ, SEQ, D])
    ys = dout("ys", [NSMP * DEC, D])
    ncv_p = dout("ncv_p", [NSEQ, HK, DC])
    ncv_s = dout("ncv_s", [NSMP, HK, DC])
    nk_p = dout("nk_p", [NSEQ, C.WIN, KVW])
    nk_s = dout("nk_s", [NSMP, C.WIN, KVW])
    nv_p = dout("nv_p", [NSEQ, C.WIN, KVW])
    nv_s = dout("nv_s", [NSMP, C.WIN, KVW])
    nf_p = dout("nf_p", [NSEQ, FK - 1, DFF])
    nf_s = dout("nf_s", [NSMP, FK - 1, DFF])


    SB0 = nc.sbuf_base
    SBTOP = nc.sbuf_top
    fixed = Arena(ceil_to(SB0, BLK), 26 * 1024)
    SLOT = 16 * 1024
    NSLOT = 3
    ringA = Arena(fixed.base + fixed.size, SLOT * NSLOT)
    H1 = Arena(ringA.base + ringA.size, 16 * 1024)
    H2 = Arena(H1.base + H1.size, 16 * 1024)
    X = Arena(H2.base + H2.size, 32 * 1024)
    Y = Arena(X.base + X.size, 52 * 1024)
    TM = Arena(Y.base + Y.size, SBTOP - (Y.base + Y.size))
    assert TM.size >= 17 * 1024, TM.size

    _n = [0]

    def sbt(arena, shape, dt, name=None):
        nbytes = int(np.prod(shape[1:])) * (4 if dt == F32 else 2)
        off = arena.take(nbytes)
        _n[0] += 1
        return P.sb("%s_%d" % (name or "t", _n[0]), shape, dt, off)

    pst = nc.alloc_psum_tensor("ps", [128, 8, 512], F32)
    P.tinfo[pst.name] = ("ps", 0, 4)
    _bank = [0]

    def bank():
        b = _bank[0] % 8
        _bank[0] += 1
        return b

    cst_f = sbt(fixed, [128, NCST], F32, "cstf")
    cst_b = sbt(fixed, [128, NCST], BF16, "cstb")
    ident_f = cst_f[:, 0:128]
    rotT_f = cst_f[:, 128:256]
    onesblk_b = cst_b[:, 256:384]
    dup_b = [cst_b[:, 384:512], cst_b[:, 512:640]]
    ones_b = sbt(fixed, [128, 128], BF16, "ones")
    oneln_b = sbt(fixed, [128, 128], BF16, "oneln")
    oneln_f = sbt(fixed, [128, 128], F32, "onelnf")
    ident_b = cst_b[:, 0:128]

    vrows = {}
    nrow = [0]

    def vreg(name, n):
        vrows[name] = nrow[0]
        nrow[0] += n
    vreg("b_in", NIN // 128)
    vreg("conv_w", CK * CC)
    vreg("conv_b", CC)
    vreg("ln_g", CC)
    vreg("ln_b", CC)
    vreg("norm1_g", FC)
    vreg("norm2_g", FC)
    vreg("ffn_conv_w", FK * FFC)
    vreg("ffn_conv_b", FFC)
    vreg("mod_b", 6 * FC)
    NVT = (nrow[0] + 127) // 128
    vecT = sbt(fixed, [128, NVT * 128], F32, "vecT")

    def vcol(name, i):
        j = vrows[name] + i
        return vecT[:, j:j + 1]

    modT = sbt(fixed, [128, 6 * FC, NSQ], F32, "modT")
    A1T = sbt(fixed, [128, FC, NSQ], F32, "A1T")
    A2T = sbt(fixed, [128, FC, NSQ], F32, "A2T")
    cT = sbt(fixed, [128, FC, NSQ], BF16, "cT")
    qkg = sbt(fixed, [128, 2], F32, "qkg")
    sinkexp = sbt(fixed, [128, NKV, 2, 64], F32, "sinkexp")
    se16 = sbt(fixed, [128, NH], F32, "se16")
    bvb = sbt(fixed, [64, KVW], F32, "bvb")
    gluH = sbt(fixed, [128, 2, CC, HK], F32, "gluH")
    gluHb = sbt(fixed, [128, 2, CC, HK], BF16, "gluHb")
    kH = sbt(fixed, [128, NKV, 128], BF16, "kH")
    vH = sbt(fixed, [64, 2, KVW], BF16, "vH")
    ffnH = sbt(fixed, [128, 2, FFC, FK - 1], F32, "ffnH")
    cosT = sbt(fixed, [128, T], F32, "cosT")
    sinT = sbt(fixed, [128, T], F32, "sinT")
    ssq = sbt(fixed, [128, 32], F32, "ssq")
    rstd_s = sbt(fixed, [128, 8], F32, "rstds")
    gbc = P.sb("gbc", [128, D], F32, Y.base + 44 * 1024)

    ring = [P.sb("ring%d" % i, [128, SLOT // 2], BF16, ringA.base + i * SLOT) for i in range(NSLOT)]

    def slab_pieces(wb, k0, nk, colranges):
        pcs = []
        c = 0
        for (c0, w) in colranges:
            pcs.append((wb[k0 * 128:(k0 + nk) * 128, c0:c0 + w], c, w))
            c += w
        return (nk, c, pcs)

    NMOD0 = 2 * FC // 4
    nmslab = 6 * D // 512

    def tile_slabs(t_index):
        L = []
        for s in range(DC // 256):
            L.append(("A", s, slab_pieces(w_in, 0, FC, [(256 * s, 256), (o_zb + 256 * s, 256)])))
        for s in range(AW // 512):
            L.append(("Q", s, slab_pieces(w_in, 0, FC, [(o_q + 512 * s, 512)])))
        L.append(("KV", 0, slab_pieces(w_in, 0, FC, [(o_k, 2 * KVW)])))
        if t_index == 0:
            for ms in range(NMOD0, nmslab):
                L.append(("MOD", ms, slab_pieces(mod_w, 0, FC, [(512 * ms, 512)])))
        for g in range(D // 512):
            L.append(("GC", g, slab_pieces(w_in, 0, FC, [(o_gc + 512 * g, 512)])))
            L.append(("CO", g, slab_pieces(conv_out_w, 0, CC, [(512 * g, 512)])))
            L.append(("GA", g, slab_pieces(w_in, 0, FC, [(o_ga + 512 * g, 512)])))
            L.append(("AO", g, slab_pieces(attn_o_w, 0, QC, [(512 * g, 512)])))
        for g in range(D // 512):
            L.append(("WO", g, slab_pieces(w_out, 0, FC, [(512 * g, 512)])))
        for s in range(DFF // 256):
            L.append(("UP", s, slab_pieces(ffn_up_w, 0, FC, [(256 * s, 256), (DFF + 256 * s, 256)])))
        for g in range(D // 512):
            k0 = 0
            while k0 < FFC:
                nk = min(16, FFC - k0)
                L.append(("DN", (g, k0), slab_pieces(ffn_down_w, k0, nk, [(512 * g, 512)])))
                k0 += nk
        return L

    n_tiles_total = NSEQ * NT + 1
    slabs = []
    scr = {}
    for t_ in range(n_tiles_total):
        for (kind, idx, desc) in tile_slabs(t_):
            if kind != "MOD" and (kind, idx) not in scr:
                scr[(kind, idx)] = dscr("scr%d" % len(scr), [128, desc[0] * desc[1]])
            slabs.append((kind, idx, desc, t_))

    class Ring:
        def __init__(self):
            self.issued = 0
            self.cons = 0

        def issue(self):
            if self.issued >= len(slabs):
                return
            i = self.issued
            kind, idx, (nk, W, pcs), t_ = slabs[i]
            slot = ring[i % NSLOT]
            v = slot[:, 0:nk * W].rearrange("p (k w) -> p k w", w=W)
            if kind == "UP":
                want = 2 + (idx % 2)
            elif kind == "DN":
                want = 4 + ((idx[0] * 3 + idx[1] // 16) % 2)
            else:
                want = 1
            store_t = max(0, min(want, n_tiles_total - 2))
            if t_ <= store_t:
                for pi, (src, c, w) in enumerate(pcs):
                    O.dma("pool", v[:, :, c:c + w], src.rearrange("(k p) w -> p k w", p=128), key="ring%d" % (i % NSLOT), chain=(pi == 0))
                if kind != "MOD" and t_ == store_t:
                    O.dma("sp", scr[(kind, idx)][:, :], slot[:, 0:nk * W], key="wst%d" % (i % NSLOT))
            else:
                O.dma("sp", slot[:, 0:nk * W], scr[(kind, idx)][:, :], key="ring%d" % (i % NSLOT))
            self.issued += 1

        def get(self, kind, idx):
            i = self.cons
            k2, i2, (nk, W, pcs), t_ = slabs[i]
            assert (k2, i2) == (kind, idx), (k2, i2, kind, idx)
            slot = ring[i % NSLOT]
            self.cons += 1
            return slot[:, 0:nk * W].rearrange("p (k w) -> p k w", w=W)

        def done(self):
            self.issue()

        def consume_mod(self, n=1, b=None):
            for _ in range(n):
                if self.cons < len(slabs) and slabs[self.cons][0] == "MOD":
                    ms = slabs[self.cons][1]
                    wv_ = self.get("MOD", ms)
                    mod_slab(ms, wv_, b)
                    self.issue()

    R = Ring()

    O.dma("sp", cst_f[:, :], cst_in[:, :], key="cst")
    O.copy(cst_b[:, :], cst_f[:, :], eng="dve")
    O.memset(ones_b[:, :], 1.0, eng="dve")
    O.memset(oneln_b[:, :], 1.0 / DC, eng="dve")
    O.memset(oneln_f[:, :], 1.0 / DC, eng="dve")

    TM.reset()
    stage = sbt(TM, [128, NVT, 128], F32, "vstage")
    O.memset(stage[:, :, :], 0.0, eng="dve")

    def stage_rows(name, src2d, n):
        r0 = vrows[name]
        done = 0
        while done < n:
            r = r0 + done
            t_, p_ = r // 128, r % 128
            m = min(n - done, 128 - p_)
            O.dma("sp", stage[p_:p_ + m, t_, :], src2d[done:done + m, :], key="vstage%d" % t_, chain=False)
            done += m
    stage_rows("b_in", b_in.rearrange("(c p) -> c p", p=128), NIN // 128)
    stage_rows("conv_w", conv_w.rearrange("k (c p) -> (k c) p", p=128), CK * CC)
    stage_rows("conv_b", conv_b.rearrange("(c p) -> c p", p=128), CC)
    stage_rows("ln_g", ln_g.rearrange("(c p) -> c p", p=128), CC)
    stage_rows("ln_b", ln_b.rearrange("(c p) -> c p", p=128), CC)
    stage_rows("norm1_g", norm1_g.rearrange("(c p) -> c p", p=128), FC)
    stage_rows("norm2_g", norm2_g.rearrange("(c p) -> c p", p=128), FC)
    stage_rows("ffn_conv_w", ffn_conv_w.rearrange("k (c p) -> (k c) p", p=128), FK * FFC)
    stage_rows("ffn_conv_b", ffn_conv_b.rearrange("(c p) -> c p", p=128), FFC)
    stage_rows("mod_b", mod_b.rearrange("(c p) -> c p", p=128), 6 * FC)
    for t_ in range(NVT):
        b = bank()
        O.tr(pst[:, b, 0:128], stage[:, t_, :], ident_f)
        O.act(vecT[:, t_ * 128:(t_ + 1) * 128], pst[:, b, 0:128], AF.Copy)

    cs = sbt(TM, [NSQ, D], F32, "cs")
    O.dma("sp", cs[:, :], call[:, :], key="cs")
    O.act(cs[:, :], cs[:, :], AF.Silu)
    b = bank()
    for fc in range(FC):
        O.tr(pst[:, b, fc * NSQ:(fc + 1) * NSQ], cs[:, fc * 128:(fc + 1) * 128], ident_f[0:NSQ, 0:NSQ])
    O.copy(cT[:, :, :], pst[:, b, 0:FC * NSQ].rearrange("p (f s) -> p f s", s=NSQ), eng="dve")

    O.dma("sp", qkg[0:64, 0:1], q_norm_g.rearrange("(p o) -> p o", o=1), key="qkg", chain=False)
    O.dma("sp", qkg[64:128, 0:1], q_norm_g.rearrange("(p o) -> p o", o=1), key="qkg", chain=False)
    O.dma("sp", qkg[0:64, 1:2], k_norm_g.rearrange("(p o) -> p o", o=1), key="qkg", chain=False)
    O.dma("sp", qkg[64:128, 1:2], k_norm_g.rearrange("(p o) -> p o", o=1), key="qkg", chain=False)
    O.dma("sp", se16[:, :], sinks.partition_broadcast(128), key="se16")
    O.dma("sp", bvb[:, :], b_in[o_v:o_v + KVW].partition_broadcast(64), key="bvb")
    O.act(se16[:, :], se16[:, :], AF.Exp)
    sev = se16[:, :].rearrange("p (g j h) -> p g j h", j=2, h=2)
    for half in range(2):
        O.copy(sinkexp[half * 64:(half + 1) * 64, :, :, :],
               sev[half * 64:(half + 1) * 64, :, :, half:half + 1].to_broadcast([64, NKV, 2, 64]), eng="dve")


    tiles = []
    for s in range(NSEQ):
        for ti in range(NT):
            tiles.append(dict(kind="p", s=s, ti=ti, T=T, L=T, nseg=1, segs=[s], first=(ti == 0), last=(ti == NT - 1)))
    tiles.append(dict(kind="s", T=TSM, L=DEC, nseg=NSMP, segs=[NSEQ + i for i in range(NSMP)], first=False, last=True))

    def x_src(tl):
        if tl["kind"] == "p":
            return xp[tl["s"], tl["ti"] * T:(tl["ti"] + 1) * T, :].rearrange("(tb p) d -> p tb d", p=128)
        return xs[:, :].rearrange("(tb p) d -> p tb d", p=128)

    def y_dst(tl):
        if tl["kind"] == "p":
            return yp[tl["s"], tl["ti"] * T:(tl["ti"] + 1) * T, :].rearrange("(tb p) d -> p tb d", p=128)
        return ys[:, :].rearrange("(tb p) d -> p tb d", p=128)

    def x_tensor(tl, arena_base, name):
        TB = tl["T"] // 128
        return P.sb(name, [128, TB, D], F32, arena_base)

    xt0 = x_tensor(tiles[0], X.base, "xt_0")
    O.dma("pool", xt0[:, :, :], x_src(tiles[0]), key="xload")

    mod_stg = [P.sb("modstg%d" % i, [NSQ, 512], F32, Y.base + 44 * 1024 + i * 2048) for i in range(2)]

    def mod_slab(ms, v, b=None):
        if b is None:
            b = bank()
        for kc in range(FC):
            O.mm(pst[0:NSQ, b, 0:512], lhsT=cT[:, kc, :], rhs=v[:, kc, :], start=(kc == 0), stop=(kc == FC - 1))
        stg = mod_stg[ms % 2]
        O.act(stg[:, :], pst[0:NSQ, b, 0:512], AF.Copy)
        for i in range(4):
            O.tr(pst[:, b, i * NSQ:(i + 1) * NSQ], stg[:, i * 128:(i + 1) * 128], ident_f[0:NSQ, 0:NSQ])
        m0 = ms * 4
        O.tt(modT[:, m0:m0 + 4, :], pst[:, b, 0:4 * NSQ].rearrange("p (m s) -> p m s", s=NSQ),
             vecT[:, vrows["mod_b"] + m0: vrows["mod_b"] + m0 + 4].unsqueeze(2).to_broadcast([128, 4, NSQ]), ALU.add)

    for ms in range(NMOD0):
        slot = ring[ms % NSLOT]
        v = slot[:, 0:FC * 512].rearrange("p (k w) -> p k w", w=512)
        O.dma("pool", v, mod_w[:, ms * 512:(ms + 1) * 512].rearrange("(k p) w -> p k w", p=128), key="ring%d" % (ms % NSLOT))
        mod_slab(ms, v)
    sh1 = modT[:, 0 * FC:1 * FC, :]
    sc1 = modT[:, 1 * FC:2 * FC, :]
    g1T = modT[:, 2 * FC:3 * FC, :]
    sh2 = modT[:, 3 * FC:4 * FC, :]
    sc2 = modT[:, 4 * FC:5 * FC, :]
    g2T = modT[:, 5 * FC:6 * FC, :]
    n1g = vecT[:, vrows["norm1_g"]:vrows["norm1_g"] + FC].unsqueeze(2).to_broadcast([128, FC, NSQ])
    n2g = vecT[:, vrows["norm2_g"]:vrows["norm2_g"] + FC].unsqueeze(2).to_broadcast([128, FC, NSQ])
    O.ts(A1T[:, :, :], sc1, 1.0, None, ALU.add)
    O.tt(A1T[:, :, :], A1T[:, :, :], n1g, ALU.mult)

    if getattr(C, "DEBUG", False):
        dbg_mod = dout("dbg_mod", [128, 6 * FC * NSQ])
        dbg_vec = dout("dbg_vec", [128, NVT * 128])
        dbg_ct = dout("dbg_ct", [128, FC * NSQ])
        ctf = sbt(TM, [128, FC * NSQ], F32, "ctf")
        O.copy(ctf[:, :], cT[:, :, :].rearrange("p f s -> p (f s)"), eng="dve")
        O.dma("pool", dbg_mod[:, :], modT[:, :, :].rearrange("p m s -> p (m s)"), key="dbg")
        O.dma("pool", dbg_vec[:, :], vecT[:, :], key="dbg")
        O.dma("pool", dbg_ct[:, :], ctf[:, :], key="dbg")

    for _ in range(NSLOT):
        R.issue()

    def seg_cols(tl, si):
        L = tl["L"]
        return slice(si * L, (si + 1) * L)

    def rsqrt_act(out, in_, scale, n):
        O.act(out, in_, AF.Ln, bias=EPS, scale=scale)
        O.act(out, out, AF.Exp, scale=-0.5)

    def norm_phase(tl, xt, xn, hT_dst, AT, shT, ssq_done=False):
        Tt, TB = tl["T"], tl["T"] // 128
        junk = sbt(TM, [128, D], BF16, "junk")
        if not ssq_done:
            for tb in range(TB):
                O.act(junk[:, :], xt[:, tb, :], AF.Square, accum_out=ssq[:, tb:tb + 1])
        rsqrt_act(rstd_s[:, 0:TB], ssq[:, 0:TB], 1.0 / D, 128)
        for tb in range(TB):
            O.ts(xn[:, tb, :], xt[:, tb, :], rstd_s[:, tb:tb + 1], None, ALU.mult)
        for fc in range(FC):
            b = bank()
            for tb in range(TB):
                O.tr(pst[:, b, tb * 128:(tb + 1) * 128], xn[:, tb, fc * 128:(fc + 1) * 128], ident_f)
            for si, sq in enumerate(tl["segs"]):
                cs_ = seg_cols(tl, si)
                if fc % 2 == 0:
                    O.act(hT_dst[:, fc, cs_], pst[:, b, cs_], AF.Identity,
                          bias=shT[:, fc, sq:sq + 1], scale=AT[:, fc, sq:sq + 1])
                else:
                    O.ts(hT_dst[:, fc, cs_], pst[:, b, cs_], AT[:, fc, sq:sq + 1], shT[:, fc, sq:sq + 1], ALU.mult, ALU.add)

    def build_gbc(tl, gT):
        Tt = tl["T"]
        _n[0] += 1
        bcin = P.sb("bcin_%d" % _n[0], [128, 4, 128], F32, Y.base + 42 * 1024)
        for f4 in range(FC // 4):
            b = bank()
            for i in range(4):
                fc = f4 * 4 + i
                for si, sq in enumerate(tl["segs"]):
                    w = 128 // tl["nseg"]
                    O.copy(bcin[:, i, si * w:(si + 1) * w], gT[:, fc, sq:sq + 1].to_broadcast([128, w]), eng="dve")
                O.tr(pst[:, b, i * 128:(i + 1) * 128], bcin[:, i, :], ident_f)
            O.act(gbc[:, f4 * 512:(f4 + 1) * 512], pst[:, b, :], AF.Copy)

    def norm1_prefetch_gen(ntl, xsrc, hT_next):
        nTB = ntl["T"] // 128
        G_ = D // 512
        pv_ = ssq[:, 8:8 + nTB * G_].rearrange("p (t g) -> p t g", g=G_)
        if G_ == 1:
            O.copy(ssq[:, 0:nTB], pv_[:, :, 0], eng="dve")
        else:
            O.tt(ssq[:, 0:nTB], pv_[:, :, 0], pv_[:, :, 1], ALU.add)
            for g_ in range(2, G_):
                O.tt(ssq[:, 0:nTB], ssq[:, 0:nTB], pv_[:, :, g_], ALU.add)
        rsqrt_act(rstd_s[:, 0:nTB], ssq[:, 0:nTB], 1.0 / D, 128)
        for tb in range(nTB):
            for g_ in range(G_):
                O.ts(xsrc[g_][:, tb, :], xsrc[g_][:, tb, :], rstd_s[:, tb:tb + 1], None, ALU.mult)
        yield
        fb = [4, 5, 6, 7]
        for fc in range(FC):
            b = fb[fc % 4]
            g_, c_ = (fc * 128) // 512, (fc * 128) % 512
            for tb in range(nTB):
                O.tr(pst[:, b, tb * 128:(tb + 1) * 128], xsrc[g_][:, tb, c_:c_ + 128], ident_f)
            for si, sq in enumerate(ntl["segs"]):
                cs_ = slice(si * ntl["L"], (si + 1) * ntl["L"])
                if fc % 2 == 0:
                    O.act(hT_next[:, fc, cs_], pst[:, b, cs_], AF.Identity, bias=sh1[:, fc, sq:sq + 1], scale=A1T[:, fc, sq:sq + 1])
                else:
                    O.ts(hT_next[:, fc, cs_], pst[:, b, cs_], A1T[:, fc, sq:sq + 1], sh1[:, fc, sq:sq + 1], ALU.mult, ALU.add)
            yield

    Hbufs = [H1, H2]
    nxt_xt = xt0
    for tix, tl in enumerate(tiles):
        Tt, L, nseg, TB = tl["T"], tl["L"], tl["nseg"], tl["T"] // 128
        NCH = L // 64
        xt = nxt_xt
        hT = P.sb("hT_%d" % tix, [128, FC, Tt], BF16, Hbufs[0].base)
        mT = P.sb("mT_%d" % tix, [128, FC, Tt], BF16, Hbufs[1].base)
        h2T = hT

        use_tm = (Tt == 128)
        tm_cache = {}
        pend = {}
        tm_stage = [P.sb("tmst%d_%d" % (i, tix), [128, 512], F32, Y.base + 36 * 1024 + i * 2048) for i in range(2)]
        tm_n = [0]

        def tm_prefetch(wv_, act, nk):
            W_ = wv_.shape[2]
            bt = bank()
            for kc in range(nk):
                O.mm(pst[:, bt, 0:W_], lhsT=act[:, kc, :], rhs=wv_[:, kc, :], start=(kc == 0), stop=(kc == nk - 1))
            stg = tm_stage[tm_n[0] % 2]
            tm_n[0] += 1
            if tm_n[0] % 2:
                O.act(stg[:, 0:W_], pst[:, bt, 0:W_], AF.Copy)
            else:
                O.copy(stg[:, 0:W_], pst[:, bt, 0:W_], eng="dve")
            tm_cache[id(wv_)] = (stg, wv_)

        def pre(kind, idx, act, nk):
            if use_tm:
                wv_ = R.get(kind, idx)
                tm_prefetch(wv_, act, nk)
                pend[(kind, idx)] = wv_

        def getp(kind, idx):
            if (kind, idx) in pend:
                return pend.pop((kind, idx))
            return R.get(kind, idx)

        def fm_group(b, wv_, c0, act, nk):
            ent = tm_cache.get(id(wv_))
            if ent is None:
                for kc in range(nk):
                    O.mm(pst[:, b, 0:Tt], lhsT=wv_[:, kc, c0:c0 + 128], rhs=act[:, kc, :], start=(kc == 0), stop=(kc == nk - 1))
            else:
                O.tr(pst[:, b, 0:Tt], ent[0][:, c0:c0 + 128], ident_f)

        if tl["kind"] == "p":
            p0 = tl["ti"] * T
            O.dma("sp", cosT[:, 0:Tt], rope_in[0, :, p0:p0 + Tt], key="cos")
            O.dma("sp", sinT[:, 0:Tt], rope_in[1, :, p0:p0 + Tt], key="sin")
        else:
            for si in range(nseg):
                O.dma("sp", cosT[:, si * L:(si + 1) * L], rope_in[0, :, SEQ:SEQ + L], key="cos", chain=(si == 0))
                O.dma("sp", sinT[:, si * L:(si + 1) * L], rope_in[1, :, SEQ:SEQ + L], key="sin", chain=(si == 0))

        if tix == 0:
            TM.reset()
            xn = P.sb("xn_%d" % tix, [128, TB, D], F32, Y.base)
            norm_phase(tl, xt, xn, hT, A1T, sh1)

        X.reset()
        Y.reset()
        qT = sbt(X, [128, QC, Tt], BF16, "qT")
        oT = sbt(X, [128, QC, Tt], BF16, "oT")
        sT = sbt(X, [128, CC, Tt], BF16, "sT")
        kT2 = sbt(X, [128, NKV, nseg, 128 + L], BF16, "kT2")
        gluT = sbt(Y, [128, CC, nseg, HK + L], BF16, "gluT")
        Vz = sbt(Y, [64, nseg, 2 + NCH, KVW], BF16, "Vz")
        dw = sbt(Y, [128, CC, Tt], F32, "dw")
        sqb = sbt(Y, [128, CC, Tt], BF16, "sqb")
        TAPG = 16
        dgb = [sbt(Y, [128, TAPG, 128], BF16, "dg0"), None]
        assert Y.cur <= Y.base + 42 * 1024, Y.cur - Y.base

        if tl["kind"] == "p":
            if tl["first"]:
                O.memset(gluHb[:, 0, :, :], 0.0, eng="pool")
                O.memset(ffnH[:, 0, :, :], 0.0, eng="pool")
            else:
                O.copy(kT2[:, :, 0, 0:128], kH[:, :, :], eng="pool")
                O.copy(Vz[:, 0, 0:2, :], vH[:, :, :], eng="pool")
        else:
            for si in range(nseg):
                TM.reset()
                st = sbt(TM, [32, DC], F32, "cst_cv")
                O.dma("pool", st[0:HK, :], cconv[si, :, :], key="h_cv")
                b = bank()
                for c in range(CC):
                    O.tr(pst[:, b, c * 32:c * 32 + HK], st[0:HK, c * 128:(c + 1) * 128], ident_f[0:HK, 0:HK])
                O.act(gluHb[:, si, :, :], pst[:, b, 0:CC * 32].rearrange("p (c k) -> p c k", k=32)[:, :, 0:HK], AF.Copy)
                stk = sbt(TM, [128, KVW], F32, "cst_k")
                O.dma("pool", stk[:, :], ck_in[si, :, :], key="h_k")
                stkd = sbt(TM, [128, NKV, 2, 64], F32, "cst_kd")
                O.copy(stkd[:, :, :, :], stk[:, :].rearrange("p (g o d) -> p g o d", o=1, d=64).to_broadcast([128, NKV, 2, 64]), eng="dve")
                for g in range(NKV):
                    b = bank()
                    O.tr(pst[:, b, 0:128], stkd[:, g, :, :].rearrange("p o d -> p (o d)"), ident_f)
                    O.act(kT2[:, g, si, 0:128], pst[:, b, 0:128], AF.Copy)
                stv = sbt(TM, [64, 2, KVW], F32, "cst_v")
                O.dma("pool", stv[:, :, :], cv_in[si, :, :].rearrange("(c p) w -> p c w", p=64), key="h_v")
                O.copy(Vz[:, si, 0:2, :], stv[:, :, :], eng="dve")
                npc = 8
                cpp = (FFC + npc - 1) // npc
                stfs = [sbt(TM, [2, cpp * 128], F32, "cst_f%d" % i) for i in range(2)]
                for pc in range(npc):
                    c0 = pc * cpp
                    c1 = min(FFC, c0 + cpp)
                    if c0 >= c1:
                        continue
                    stf = stfs[pc % 2]
                    O.dma("pool", stf[:, 0:(c1 - c0) * 128], cffn[si, :, c0 * 128:c1 * 128], key="h_f%d" % (pc % 2))
                    b = bank()
                    for c in range(c0, c1):
                        O.tr(pst[:, b, (c - c0) * 2:(c - c0) * 2 + 2], stf[:, (c - c0) * 128:(c - c0 + 1) * 128], ident_f[0:2, 0:2])
                    O.act(ffnH[:, si, c0:c1, :], pst[:, b, 0:(c1 - c0) * 2].rearrange("p (c k) -> p c k", k=2), AF.Copy)
                O.dma("pool", nk_s[si, 0:C.WIN - DEC, :], ck_in[si, DEC:C.WIN, :], key="d2d")
                O.dma("pool", nv_s[si, 0:C.WIN - DEC, :], cv_in[si, DEC:C.WIN, :], key="d2d")
        for si in range(nseg):
            O.copy(gluT[:, :, si, 0:HK], gluHb[:, si, :, :], eng="pool")

        TM.reset()
        Hbufs[1].reset()
        HT = Hbufs[1]
        tA = [sbt(TM, [128, Tt], F32, "tA0"), sbt(HT, [128, Tt], F32, "tA1")]
        dgb[1] = sbt(TM, [128, TAPG, 128], BF16, "dg1")
        cnt = [0]

        def seg_view(ap2d):
            return ap2d.rearrange("p (s l) -> p s l", l=L)

        bg = []

        def pump(n):
            for _ in range(n):
                if not bg:
                    return
                bg.pop(0)()

        for s in range(DC // 256):
            wv = R.get("A", s)
            for i in range(2):
                c = 2 * s + i
                ba, bb = bank(), bank()
                for kc in range(FC):
                    O.mm(pst[:, ba, 0:Tt], lhsT=wv[:, kc, i * 128:(i + 1) * 128], rhs=hT[:, kc, :], start=(kc == 0), stop=(kc == FC - 1))
                for kc in range(FC):
                    O.mm(pst[:, bb, 0:Tt], lhsT=wv[:, kc, 256 + i * 128:256 + (i + 1) * 128], rhs=hT[:, kc, :], start=(kc == 0), stop=(kc == FC - 1))
                sg = tA[cnt[0] % 2]
                cnt[0] += 1
                O.act(sg[:, :], pst[:, bb, 0:Tt], AF.Sigmoid, bias=vcol("b_in", (o_zb // 128) + c))
                O.stt(gluT[:, c, :, HK:HK + L], seg_view(pst[:, ba, 0:Tt]), vcol("b_in", c), seg_view(sg[:, :]), ALU.add, ALU.mult)
                if tl["last"]:
                    O.stt(gluH[:, 0:nseg, c, :], seg_view(pst[:, ba, 0:Tt])[:, :, L - HK:L], vcol("b_in", c),
                          seg_view(sg[:, :])[:, :, L - HK:L], ALU.add, ALU.mult)
            R.done()
        CONV_BANKS = [6, 7]

        def conv_gen():
            groups = [(c, k0) for c in range(CC) for k0 in range(0, CK, TAPG)]

            def build(gi):
                c, k0 = groups[gi]
                n = min(TAPG, CK - k0)
                j0 = vrows["conv_w"] + k0 * CC + c
                wtap = vecT[:, j0:j0 + (n - 1) * CC + 1:CC].unsqueeze(2).to_broadcast([128, n, 128])
                O.tt(dgb[gi % 2][:, 0:n, :], ident_b.unsqueeze(1).to_broadcast([128, n, 128]), wtap, ALU.mult)

            build(0)
            for gi, (c, k0) in enumerate(groups):
                if nseg == 1:
                    bks = [CONV_BANKS[c % 2]]
                else:
                    bks = CONV_BANKS[:nseg]
                n = min(TAPG, CK - k0)
                dg = dgb[gi % 2]
                for k in range(k0, k0 + n):
                    for si in range(nseg):
                        O.mm(pst[:, bks[si], 0:L], lhsT=dg[:, k - k0, :], rhs=gluT[:, c, si, k:k + L],
                             start=(k == 0), stop=(k == CK - 1))
                    if k == k0 and gi + 1 < len(groups):
                        build(gi + 1)
                    yield
                if k0 + n >= CK:
                    for si in range(nseg):
                        O.act(dw[:, c, si * L:(si + 1) * L], pst[:, bks[si], 0:L], AF.Identity, bias=vcol("conv_b", c))
                        O.act(sqb[:, c, si * L:(si + 1) * L], pst[:, bks[si], 0:L], AF.Square, bias=vcol("conv_b", c))
            for si in range(nseg):
                O.copy(gluHb[:, si, :, :], gluT[:, :, si, L:L + HK], eng="pool")

        last = tl["last"]
        if last:
            kst = sbt(TM, [128, KVW], F32, "kst")

        wv_kv = [None]
        kf = [sbt(TM, [128, Tt], F32, "kf%d" % i) for i in range(KC2)]
        kb = [sbt(HT, [128, Tt], BF16, "kb%d" % i) for i in range(KC2)]
        raws = [sbt(HT, [128, Tt], F32, "qraw%d" % i) for i in range(3)]
        rss = [sbt(HT, [128, Tt], F32, "qrs%d" % i) for i in range(2)]
        sqs = [sbt(HT, [128, Tt], BF16, "qsq%d" % i) for i in range(2)]
        nq = QC
        chunks = []
        for j in range(QC):
            chunks.append(("q", j, vcol("b_in", o_q // 128 + j), qkg[:, 0:1], qT[:, j, :], None))
        for j in range(KC2):
            chunks.append(("k", j, vcol("b_in", o_k // 128 + j), qkg[:, 1:2], kb[j][:, :], kf[j][:, :]))
        nchunk = len(chunks)
        st = {}
        wq = [None]

        def stage_a(n):
            kind, j, bcol, gcol, dst_bf, dst_f32 = chunks[n]
            if kind == "q":
                if j % 4 == 0:
                    wq[0] = R.get("Q", j // 4)
                wv, c0 = wq[0], (j % 4) * 128
            else:
                if j == 0:
                    wv_kv[0] = R.get("KV", 0)
                wv, c0 = wv_kv[0], j * 128
            b = bank()
            for kc in range(FC):
                O.mm(pst[:, b, 0:Tt], lhsT=wv[:, kc, c0:c0 + 128], rhs=hT[:, kc, :], start=(kc == 0), stop=(kc == FC - 1))
            if kind == "q" and j % 4 == 3:
                R.done()
            raw, sq = raws[n % 3], sqs[n % 2]
            O.act(raw[:, :], pst[:, b, 0:Tt], AF.Identity, bias=bcol)
            O.act(sq[:, :], pst[:, b, 0:Tt], AF.Square, bias=bcol)

        def stage_b(n):
            kind, j, bcol, gcol, dst_bf, dst_f32 = chunks[n]
            raw, sq, rs = raws[n % 3], sqs[n % 2], rss[n % 2]
            b2 = bank()
            O.mm(pst[:, b2, 0:Tt], lhsT=onesblk_b, rhs=sq[:, :])
            rsqrt_act(rs[:, :], pst[:, b2, 0:Tt], 1.0, 128)
            O.stt(raw[:, :], raw[:, :], gcol, rs[:, :], ALU.mult, ALU.mult)

        def stage_c(n):
            kind, j, bcol, gcol, dst_bf, dst_f32 = chunks[n]
            qn, t1 = raws[n % 3], rss[n % 2]
            b3 = bank()
            O.mm(pst[:, b3, 0:Tt], lhsT=rotT_f, rhs=qn[:, :])
            O.tt(t1[:, :], qn[:, :], cosT[:, 0:Tt], ALU.mult)
            O.tt(qn[:, :], pst[:, b3, 0:Tt], sinT[:, 0:Tt], ALU.mult)
            if dst_f32 is None:
                O.tt(dst_bf, t1[:, :], qn[:, :], ALU.add)
            else:
                O.tt(dst_f32, t1[:, :], qn[:, :], ALU.add)
                O.act(dst_bf, dst_f32, AF.Copy)

        for step in range(nchunk + 2):
            if step < nchunk:
                stage_a(step)
            if 0 <= step - 1 < nchunk:
                stage_b(step - 1)
            if 0 <= step - 2 < nchunk:
                stage_c(step - 2)
        wv = wv_kv[0]
        for g in range(NKV):
            b = bank()
            O.mm(pst[:, b, 0:Tt], lhsT=dup_b[g % 2], rhs=kb[g // 2][:, :])
            O.act(kT2[:, g, :, 128:128 + L], seg_view(pst[:, b, 0:Tt]), AF.Copy)
        if last:
            for si in range(nseg):
                nrows = min(C.WIN, L)
                b = bank()
                for j in range(KC2):
                    O.tr(pst[0:nrows, b, j * 128:(j + 1) * 128], kf[j][:, (si + 1) * L - nrows:(si + 1) * L], ident_f)
                O.act(kst[0:nrows, :], pst[0:nrows, b, 0:KVW], AF.Copy)
                if tl["kind"] == "p":
                    O.dma("pool", nk_p[tl["s"], :, :], kst[0:nrows, :], key="kst")
                else:
                    O.dma("pool", nk_s[si, C.WIN - DEC:C.WIN, :], kst[0:nrows, :], key="kst")
        if last:
            vst = sbt(TM, [64, 2, KVW], F32, "vst")
        for si in range(nseg):
            for ch in range(NCH):
                b = bank()
                c0 = si * L + ch * 64
                for kc in range(FC):
                    O.mm(pst[0:64, b, 0:KVW], lhsT=hT[:, kc, c0:c0 + 64], rhs=wv[:, kc, KVW:2 * KVW], start=(kc == 0), stop=(kc == FC - 1))
                O.tt(Vz[:, si, 2 + ch, :], pst[0:64, b, 0:KVW], bvb[:, :], ALU.add)
                if last and ch >= NCH - 2:
                    slot_ = ch - (NCH - 2) if NCH >= 2 else 0
                    O.tt(vst[:, slot_, :], pst[0:64, b, 0:KVW], bvb[:, :], ALU.add)
                pump(3)
            if last:
                if tl["kind"] == "p":
                    O.dma("pool", nv_p[tl["s"], :, :].rearrange("(c p) w -> p c w", p=64), vst[:, :, :], key="vst")
                else:
                    O.dma("pool", nv_s[si, C.WIN - DEC:C.WIN, :], vst[:, 0, :], key="vst")
        R.done()

        pT_all = sbt(TM, [64, 4, 3, 2, 64], BF16, "pT")
        pT = [[pT_all[:, 2 * i + h, :, :, :] for h in range(2)] for i in range(2)]
        rden = [sbt(TM, [128, 2, 64], F32, "rden%d" % i) for i in range(2)]
        its = []
        for si in range(nseg):
            for ci in range(NCH):
                kk_list = [ci, ci + 1, ci + 2]
                if tl["kind"] == "p" and tl["first"]:
                    kk_list = [k_ for k_ in kk_list if k_ >= 2]
                for g in range(NKV):
                    its.append((si, ci, g, kk_list))

        ATT_BANKS = [[0, 1, 2], [3, 4, 5]]
        pTv = [pT_all[:, 2 * i:2 * i + 2, :, :, :] for i in range(2)]

        def att_scores(n):
            si, ci, g, kk_list = its[n]
            par = n % 2
            nk = len(kk_list)
            qcols = slice(si * L + ci * 64, si * L + (ci + 1) * 64)
            for half in range(2):
                hp = slice(half * 64, (half + 1) * 64)
                bS = ATT_BANKS[par][half]
                psS = pst[0:64, bS, 0:384].rearrange("p (k j q) -> p k j q", j=2, q=64)
                for ks, kk in enumerate(kk_list):
                    O.mm(psS[:, ks, :, :], lhsT=kT2[hp, g, si, kk * 64:(kk + 1) * 64],
                         rhs=qT[hp, 2 * g:2 * g + 2, qcols])
                O.act(pT[par][half][:, 0:nk, :, :], psS[:, 0:nk, :, :], AF.Exp, scale=HD ** -0.5)

        def att_pv(n):
            si, ci, g, kk_list = its[n]
            par = n % 2
            nk = len(kk_list)
            qcols = slice(si * L + ci * 64, si * L + (ci + 1) * 64)
            bO = ATT_BANKS[par][2]
            psO = pst[:, bO, 0:128].rearrange("p (j q) -> p j q", q=64)
            psD = pst[:, bO, 128:384].rearrange("p (h j q) -> p h j q", h=2, q=64)
            for half in range(2):
                hp = slice(half * 64, (half + 1) * 64)
                for ks, kk in enumerate(kk_list):
                    O.mm(psO[hp, :, :], lhsT=Vz[:, si, kk, g * 64:(g + 1) * 64], rhs=pT[par][half][:, ks, :, :],
                         start=(ks == 0), stop=(ks == nk - 1))
            for ks, kk in enumerate(kk_list):
                O.mm(psD, lhsT=ones_b[0:64, :], rhs=pTv[par][:, :, ks, :, :], start=(ks == 0), stop=(ks == nk - 1))
            for half in range(2):
                hp = slice(half * 64, (half + 1) * 64)
                O.tt(rden[par][hp, :, :], psD[hp, half, :, :], sinkexp[hp, g, :, :], ALU.add)
            O.recip(rden[par][:, :, :], rden[par][:, :, :])
            O.tt(oT[:, 2 * g:2 * g + 2, qcols], psO, rden[par][:, :, :], ALU.mult)

        cg = conv_gen()
        n_conv_steps = CC * CK
        per_it = (n_conv_steps + len(its) - 1) // len(its)
        att_scores(0)
        for n in range(len(its)):
            if n + 1 < len(its):
                att_scores(n + 1)
            for _ in range(per_it):
                next(cg, None)
            att_pv(n)
            if tix == 0 and n % 2 == 1:
                R.consume_mod(1, b=ATT_BANKS[(n + 1) % 2][2])
        for _ in cg:
            pass
        if tix == 0:
            R.consume_mod(nmslab)
        if tl["kind"] == "p" and not tl["last"]:
            O.copy(kH[:, :, :], kT2[:, :, 0, L:L + 128], eng="pool")
            O.copy(vH[:, :, :], Vz[:, 0, NCH:NCH + 2, :], eng="pool")
        pump(10 ** 6)

        if last:
            TM.reset()
            for si in range(nseg):
                cvst = sbt(TM, [32, DC], F32, "cvst")
                for c4 in range((CC + 3) // 4):
                    b = bank()
                    n4 = min(4, CC - c4 * 4)
                    for i in range(n4):
                        c = c4 * 4 + i
                        O.tr(pst[0:HK, b, i * 128:(i + 1) * 128], gluH[:, si, c, :], ident_f)
                    O.act(cvst[0:HK, c4 * 512:c4 * 512 + n4 * 128], pst[0:HK, b, 0:n4 * 128], AF.Copy)
                if tl["kind"] == "p":
                    O.dma("pool", ncv_p[tl["s"], :, :], cvst[0:HK, :], key="cvst%d" % si)
                else:
                    O.dma("pool", ncv_s[si, :, :], cvst[0:HK, :], key="cvst%d" % si)

        TM.reset()
        mean_sb = sbt(TM, [128, Tt], F32, "mean")
        var_sb = sbt(TM, [128, Tt], F32, "var")
        rstd_l = sbt(TM, [128, Tt], F32, "rstdl")
        nmr = sbt(TM, [128, Tt], F32, "nmr")
        tt_ = [sbt(TM, [128, Tt], F32, "lnt%d" % i) for i in range(2)]
        bm, bx = bank(), bank()
        for c in range(CC):
            O.mm(pst[:, bm, 0:Tt], lhsT=oneln_f[:, :], rhs=dw[:, c, :], start=(c == 0), stop=(c == CC - 1))
        for c in range(CC):
            O.mm(pst[:, bx, 0:Tt], lhsT=oneln_b[:, :], rhs=sqb[:, c, :], start=(c == 0), stop=(c == CC - 1))
        O.act(mean_sb[:, :], pst[:, bm, 0:Tt], AF.Copy)
        O.act(var_sb[:, :], pst[:, bm, 0:Tt], AF.Square)
        O.tt(var_sb[:, :], pst[:, bx, 0:Tt], var_sb[:, :], ALU.subtract)
        rsqrt_act(rstd_l[:, :], var_sb[:, :], 1.0, 128)
        O.stt(nmr[:, :], mean_sb[:, :], -1.0, rstd_l[:, :], ALU.mult, ALU.mult)
        for c in range(CC):
            t_ = tt_[c % 2]
            O.tt(t_[:, :], dw[:, c, :], rstd_l[:, :], ALU.mult)
            O.tt(t_[:, :], t_[:, :], nmr[:, :], ALU.add)
            O.act(sT[:, c, :], t_[:, :], AF.Silu, bias=vcol("ln_b", c), scale=vcol("ln_g", c))

        xres = P.sb("xres_%d" % tix, [128, TB, D], F32, Y.base)
        O.dma("pool", xres[:, :, :], x_src(tl), key="xres")

        TM.reset()
        sgt = sbt(TM, [128, 4, Tt], F32, "sgt")
        m1t = sbt(TM, [128, 4, Tt], F32, "m1t")
        m2t = sbt(X, [128, Tt], F32, "m2t")
        for g in range(D // 512):
            wv = getp("GC", g)
            pre("CO", g, sT, CC)
            for i in range(4):
                j = 4 * g + i
                b = bank()
                fm_group(b, wv, i * 128, hT, FC)
                O.act(sgt[:, i, :], pst[:, b, 0:Tt], AF.Sigmoid, bias=vcol("b_in", o_gc // 128 + j))
            R.done()
            wv = getp("CO", g)
            pre("GA", g, hT, FC)
            for i in range(4):
                b = bank()
                fm_group(b, wv, i * 128, sT, CC)
                O.tt(m1t[:, i, :], pst[:, b, 0:Tt], sgt[:, i, :], ALU.mult)
            R.done()
            wv = getp("GA", g)
            pre("AO", g, oT, QC)
            for i in range(4):
                j = 4 * g + i
                b = bank()
                fm_group(b, wv, i * 128, hT, FC)
                O.act(sgt[:, i, :], pst[:, b, 0:Tt], AF.Sigmoid, bias=vcol("b_in", o_ga // 128 + j))
            R.done()
            wv = getp("AO", g)
            if g + 1 < D // 512:
                pre("GC", g + 1, hT, FC)
            for i in range(4):
                j = 4 * g + i
                b = bank()
                fm_group(b, wv, i * 128, oT, QC)
                O.tt(m2t[:, :], pst[:, b, 0:Tt], sgt[:, i, :], ALU.mult)
                O.tt(mT[:, j, :], m1t[:, i, :], m2t[:, :], ALU.add)
            R.done()
            if g == 0:
                build_gbc(tl, g1T)

        TM.reset()
        ep = [sbt(TM, [128, 512], F32, "ep%d" % i) for i in range(2)]
        sjunk = sbt(TM, [128, 512], BF16, "sjunk")
        ssqp = sbt(TM, [128, TB, D // 512], F32, "ssqp")
        x1 = P.sb("x1_%d" % tix, [128, TB, D], F32, X.base)
        e_i = 0
        for g in range(D // 512):
            wv = R.get("WO", g)
            bks = [bank() for _ in range(TB)]
            for tb in range(TB):
                for kc in range(FC):
                    O.mm(pst[:, bks[tb], :], lhsT=mT[:, kc, tb * 128:(tb + 1) * 128], rhs=wv[:, kc, :], start=(kc == 0), stop=(kc == FC - 1))
                e = ep[e_i % 2]
                e_i += 1
                O.tt(e[:, :], pst[:, bks[tb], :], gbc[:, g * 512:(g + 1) * 512], ALU.mult)
                O.tt(x1[:, tb, g * 512:(g + 1) * 512], e[:, :], xres[:, tb, g * 512:(g + 1) * 512], ALU.add)
                O.act(sjunk[:, :], x1[:, tb, g * 512:(g + 1) * 512], AF.Square, accum_out=ssqp[:, tb, g:g + 1])
            R.done()
        G_ = D // 512
        if G_ == 1:
            O.copy(ssq[:, 0:TB], ssqp[:, :, 0], eng="dve")
        else:
            O.tt(ssq[:, 0:TB], ssqp[:, :, 0], ssqp[:, :, 1], ALU.add)
            for g_ in range(2, G_):
                O.tt(ssq[:, 0:TB], ssq[:, 0:TB], ssqp[:, :, g_], ALU.add)

        TM.reset()
        if tix == 0:
            assert all(k_[0] != "MOD" for k_ in slabs[R.cons:]), "mod slabs must be consumed before norm2"
            O.ts(A2T[:, :, :], sc2, 1.0, None, ALU.add)
            O.tt(A2T[:, :, :], A2T[:, :, :], n2g, ALU.mult)
        xn2 = P.sb("xn2_%d" % tix, [128, TB, D], F32, Y.base)
        norm_phase(tl, x1, xn2, h2T, A2T, sh2, ssq_done=True)
        build_gbc(tl, g2T)

        TM.reset()
        aT = P.sb("aT_%d" % tix, [128, FFC, Tt], BF16, Y.base)
        assert FFC * Tt * 2 <= 44 * 1024
        usb = [sbt(TM, [128, nseg, FK - 1 + L], F32, "usb%d" % i) for i in range(2)]
        acc_ = [sbt(TM, [128, Tt], F32, "facc%d" % i) for i in range(2)]
        sg_ = [sbt(TM, [128, Tt], F32, "fsg%d" % i) for i in range(2)]
        for s in range(DFF // 256):
            if s == 0:
                pre("UP", 0, h2T, FC)
            wv = getp("UP", s)
            if s + 1 < DFF // 256:
                pre("UP", s + 1, h2T, FC)
            for i in range(2):
                c = 2 * s + i
                pr = c % 2
                bg_, bv_ = bank(), bank()
                fm_group(bg_, wv, i * 128, h2T, FC)
                fm_group(bv_, wv, 256 + i * 128, h2T, FC)
                u = usb[pr]
                a = acc_[pr]
                av = seg_view(a[:, :])
                O.copy(u[:, :, 0:FK - 1], ffnH[:, 0:nseg, c, :], eng="dve")
                O.act(u[:, :, FK - 1:FK - 1 + L], seg_view(pst[:, bg_, 0:Tt]), AF.Copy)
                O.act(a[:, :], pst[:, bg_, 0:Tt], AF.Identity, bias=vcol("ffn_conv_b", c), scale=vcol("ffn_conv_w", (FK - 1) * FFC + c))
                for k in range(FK - 1):
                    O.stt(av, u[:, :, k:k + L], vcol("ffn_conv_w", k * FFC + c), av, ALU.mult, ALU.add)
                O.copy(ffnH[:, 0:nseg, c, :], u[:, :, L:L + FK - 1], eng="dve")
                O.act(sg_[pr][:, :], a[:, :], AF.Silu)
                O.tt(aT[:, c, :], sg_[pr][:, :], pst[:, bv_, 0:Tt], ALU.mult)
            R.done()
        if last:
            fst = [sbt(TM, [FK - 1, 512], F32, "fst%d" % i) for i in range(2)]
            fi = 0
            for si in range(nseg):
                for c4 in range((FFC + 3) // 4):
                    n4 = min(4, FFC - c4 * 4)
                    b = bank()
                    for i in range(n4):
                        O.tr(pst[0:FK - 1, b, i * 128:(i + 1) * 128], ffnH[:, si, c4 * 4 + i, :], ident_f)
                    f_ = fst[fi % 2]
                    fi += 1
                    O.act(f_[:, 0:n4 * 128], pst[0:FK - 1, b, 0:n4 * 128], AF.Copy)
                    dst = nf_p[tl["s"]] if tl["kind"] == "p" else nf_s[si]
                    O.dma("pool", dst[:, c4 * 512:c4 * 512 + n4 * 128], f_[:, 0:n4 * 128], key="fst%d" % (fi % 2))

        TM.reset()
        ep = [sbt(TM, [128, 512], F32, "epd%d" % i) for i in range(2)]
        sjunk1 = sbt(TM, [128, 512], BF16, "sjunk1")
        has_next = tix + 1 < len(tiles)
        G_ = D // 512
        filler = None
        side = {}
        if has_next:
            ntl = tiles[tix + 1]
            nTB = ntl["T"] // 128
            nxt_xt = x_tensor(ntl, X.base, "xt_%d" % (tix + 1))
            hT_next = P.sb("hTn_%d" % tix, [128, FC, ntl["T"]], BF16, Hbufs[0].base)
            xsrc = {}
            side = {}
            side[G_ - 1] = sbt(TM, [128, nTB, 512], F32, "xtail")
            if G_ >= 2:
                side[G_ - 2] = P.sb("xtail2_%d" % tix, [128, nTB, 512], F32, Hbufs[1].base)
            for g_ in range(G_):
                xsrc[g_] = side[g_][:, :, :] if g_ in side else nxt_xt[:, :, g_ * 512:(g_ + 1) * 512]
            for g_ in sorted(side):
                O.dma("pool", xsrc[g_], x_src(ntl)[:, :, g_ * 512:(g_ + 1) * 512], key="xl%d" % g_)
                for tb in range(nTB):
                    j_ = 8 + tb * G_ + g_
                    O.act(sjunk1[:, :], xsrc[g_][:, tb, :], AF.Square, accum_out=ssq[:, j_:j_ + 1])
            filler = norm1_prefetch_gen(ntl, xsrc, hT_next)
        for g in range(G_):
            gc_ = slice(g * 512, (g + 1) * 512)
            lastg = (g == G_ - 1)
            tot_mm = FFC * TB
            fill_start = tot_mm // 5
            fill_every = max(1, (tot_mm - fill_start - 8) // (FC + 1))
            if lastg:
                bks = list(range(TB))
            else:
                bks = [bank() for _ in range(TB)]
            k0 = 0
            nmm = 0
            while k0 < FFC:
                nk = min(16, FFC - k0)
                wv = R.get("DN", (g, k0))
                for tb in range(TB):
                    for kc in range(nk):
                        O.mm(pst[:, bks[tb], :], lhsT=aT[:, k0 + kc, tb * 128:(tb + 1) * 128], rhs=wv[:, kc, :],
                             start=(k0 + kc == 0), stop=(k0 + kc == FFC - 1))
                        nmm += 1
                        if lastg and filler is not None:
                            if nmm == 1 or (nmm >= fill_start and (nmm - fill_start) % fill_every == 0):
                                next(filler, None)
                R.done()
                k0 += nk
            if lastg and filler is not None:
                for _ in filler:
                    pass
            for tb in range(TB):
                e = ep[e_i % 2]
                e_i += 1
                O.tt(e[:, :], pst[:, bks[tb], :], gbc[:, gc_], ALU.mult)
                O.tt(x1[:, tb, gc_], e[:, :], x1[:, tb, gc_], ALU.add)
            O.dma("pool", y_dst(tl)[:, :, gc_], x1[:, :, gc_], key="ys%d" % g)
            if has_next and g not in side:
                O.dma("pool", xsrc[g], x_src(ntl)[:, :, gc_], key="xl%d" % g)
                for tb in range(nTB):
                    j_ = 8 + tb * G_ + g
                    O.act(sjunk1[:, :], xsrc[g][:, tb, :], AF.Square, accum_out=ssq[:, j_:j_ + 1])

    P.finalize(final_wait_eng="pool")
    return nc, P


def host_consts(C):
    cst = np.zeros((128, 5 * 128), np.float32)
    cst[:, 0:128] = np.eye(128, dtype=np.float32)
    rot = np.zeros((128, 128), np.float32)
    for m in range(128):
        d = m % 64
        base = m - d
        if d < 32:
            rot[base + d + 32, m] = -1.0
        else:
            rot[base + d - 32, m] = 1.0
    cst[:, 128:256] = rot
    blk = np.zeros((128, 128), np.float32)
    blk[0:64, 0:64] = 1.0 / 64
    blk[64:128, 64:128] = 1.0 / 64
    cst[:, 256:384] = blk
    for h in range(2):
        dp = np.zeros((128, 128), np.float32)
        for m in range(128):
            dp[h * 64 + (m % 64), m] = 1.0
        cst[:, 384 + 128 * h:512 + 128 * h] = dp
    half = C.HD // 2
    inv_freq = (1.0 / (np.float32(C.THETA) ** (np.arange(half, dtype=np.float32) / np.float32(half)))).astype(np.float32)
    pos = np.concatenate([np.arange(C.SEQ, dtype=np.float32), C.PAST + np.arange(C.DEC, dtype=np.float32)])
    ang = (pos[None, :] * inv_freq[:, None]).astype(np.float32)
    idx = (np.arange(128) % 64) % 32
    rope = np.stack([np.cos(ang)[idx], np.sin(ang)[idx]]).astype(np.float32)
    return cst, rope


_CACHE = {}


def kernel(**inputs):
    C = Cfg
    f = lambda k: np.ascontiguousarray(np.asarray(inputs[k], dtype=np.float32))
    if "nc" not in _CACHE:
        _CACHE["nc"] = build_program(C)[0]
    nc = _CACHE["nc"]
    cst, rope = host_consts(C)
    x_prompt, x_sample = f("x_prompt"), f("x_sample")
    c_prompt, c_sample = f("c_prompt"), f("c_sample")
    cache_conv, cache_k, cache_v, cache_ffn = f("cache_conv")[0], f("cache_k")[0], f("cache_v")[0], f("cache_ffn_conv")[0]
    shared = {
        "mod_w": f("mod_w")[0], "mod_b": f("mod_b")[0], "norm1_g": f("norm1_g")[0], "w_in": f("w_in")[0],
        "b_in": f("b_in")[0], "conv_w": f("conv_w")[0], "conv_b": f("conv_b")[0], "ln_g": f("ln_g")[0],
        "ln_b": f("ln_b")[0], "conv_out_w": f("conv_out_w")[0], "q_norm_g": f("q_norm_g")[0],
        "k_norm_g": f("k_norm_g")[0], "sinks": f("sinks")[0], "attn_o_w": f("attn_o_w")[0], "w_out": f("w_out")[0],
        "norm2_g": f("norm2_g")[0], "ffn_up_w": f("ffn_up_w")[0], "ffn_conv_w": f("ffn_conv_w")[0],
        "ffn_conv_b": f("ffn_conv_b")[0], "ffn_down_w": f("ffn_down_w")[0], "cst": cst, "rope": rope,
    }
    NS, NM = C.NSEQ, C.NSMP
    in_maps = []
    for c in range(N_CORES):
        m = dict(shared)
        m["xp"] = x_prompt[c * NS:(c + 1) * NS]
        m["xs"] = x_sample[c * NM:(c + 1) * NM].reshape(NM * C.DEC, C.D)
        m["call"] = np.concatenate([c_prompt[c * NS:(c + 1) * NS], c_sample[c * NM:(c + 1) * NM]], axis=0)
        m["cconv"] = cache_conv[c * NM:(c + 1) * NM]
        m["ck"] = cache_k[c * NM:(c + 1) * NM].reshape(NM, C.WIN, C.NKV * C.HD)
        m["cv"] = cache_v[c * NM:(c + 1) * NM].reshape(NM, C.WIN, C.NKV * C.HD)
        m["cffn"] = cache_ffn[c * NM:(c + 1) * NM]
        in_maps.append({k: np.ascontiguousarray(v) for k, v in m.items()})
    res = run_bass_kernel_spmd(nc, in_maps, core_ids=list(range(N_CORES)))
    rs = res.results
    cat = lambda k: np.concatenate([np.asarray(r[k]) for r in rs], axis=0)
    B, BS = N_CORES * NS, N_CORES * NM
    y_p = cat("yp").reshape(B, C.SEQ, C.D)
    y_s = cat("ys").reshape(BS, C.DEC, C.D)
    return (
        y_p.astype(np.float32), y_s.astype(np.float32),
        cat("ncv_p").reshape(1, B, C.CK - 1, C.DC), cat("ncv_s").reshape(1, BS, C.CK - 1, C.DC),
        cat("nk_p").reshape(1, B, C.WIN, C.NKV, C.HD), cat("nk_s").reshape(1, BS, C.WIN, C.NKV, C.HD),
        cat("nv_p").reshape(1, B, C.WIN, C.NKV, C.HD), cat("nv_s").reshape(1, BS, C.WIN, C.NKV, C.HD),
        cat("nf_p").reshape(1, B, C.FK - 1, C.DFF), cat("nf_s").reshape(1, BS, C.FK - 1, C.DFF),
    )
```
